# Optimizing a Trainium2 kernel written in Bass

```python
import jax, jax.numpy as jnp
from jax import lax
import numpy as np

D_MODEL = 1024
BATCH = 8
SEQ = 4096
DEPTH = 1
DEC_BATCH = 128
DEC_SEQ = 1
PAST_LEN = 16384
PAGE_SIZE = 128

HEAD_DIM = 64
SWA_HEADS = 8
SWA_KV_HEADS = 2
SWA_GROUP = SWA_HEADS // SWA_KV_HEADS
WINDOW = 128
SWA_BLOCK = WINDOW
N_MEM = 256
MEM_HEADS = 4
RNN_WIDTH = 256
RNN_BLOCKS = 4
RNN_BLOCK_DIM = RNN_WIDTH // RNN_BLOCKS
CONV_WIDTH = 4
RG_C = 8.0
D_FF = 2816
EPS = 1e-6
NEG = -1e30

Q_SWA = SWA_HEADS * HEAD_DIM
KV_SWA = SWA_KV_HEADS * HEAD_DIM
Q_MEM = MEM_HEADS * HEAD_DIM
D_IN = Q_SWA + 2 * KV_SWA + Q_MEM + 2 * RNN_WIDTH
D_MIX = Q_SWA + Q_MEM + RNN_WIDTH
IN_SPLITS = (Q_SWA, Q_SWA + KV_SWA, Q_SWA + 2 * KV_SWA, Q_SWA + 2 * KV_SWA + Q_MEM,
             Q_SWA + 2 * KV_SWA + Q_MEM + RNN_WIDTH)

kernel_name = 'hymba_swa_rglru_memxattn_macaron_step'


def rms_norm(x, g):
    xf = x.astype(jnp.float32)
    y = xf * lax.rsqrt(jnp.mean(xf * xf, axis=-1, keepdims=True) + EPS)
    return (y * g.astype(jnp.float32)).astype(x.dtype)


def swiglu(x, w_in, w_out):
    gate, up = jnp.split(x @ w_in, 2, axis=-1)
    return (jax.nn.silu(gate) * up) @ w_out


def sink_softmax(s, sink, mask):
    s = jnp.where(mask, s, NEG)
    sink = sink.astype(jnp.float32)[:, :, None, None]
    lse = jnp.logaddexp(jax.nn.logsumexp(s, axis=-1, keepdims=True), sink)
    return jnp.exp(s - lse)


def swa_prompt(q, k, v, sinks):
    B, T = q.shape[:2]
    nb = T // SWA_BLOCK
    S = SWA_BLOCK
    qb = q.reshape(B, nb, S, SWA_KV_HEADS, SWA_GROUP, HEAD_DIM)
    kb = k.reshape(B, nb, S, SWA_KV_HEADS, HEAD_DIM)
    vb = v.reshape(B, nb, S, SWA_KV_HEADS, HEAD_DIM)
    pad = jnp.zeros_like(kb[:, :1])
    kk = jnp.concatenate([jnp.concatenate([pad, kb[:, :-1]], axis=1), kb], axis=2)
    vv = jnp.concatenate([jnp.concatenate([pad, vb[:, :-1]], axis=1), vb], axis=2)
    s = jnp.einsum('bnqkgd,bnskd->bnkgqs', qb, kk).astype(jnp.float32) * (HEAD_DIM ** -0.5)
    rel = (jnp.arange(S)[:, None] + S) - jnp.arange(2 * S)[None, :]
    band = (rel >= 0) & (rel <= WINDOW)
    blk_ok = (jnp.arange(nb)[:, None, None] > 0) | (jnp.arange(2 * S)[None, None, :] >= S)
    mask = (band[None] & blk_ok)[None, :, None, None]
    p = sink_softmax(s, sinks.reshape(SWA_KV_HEADS, SWA_GROUP), mask)
    o = jnp.einsum('bnkgqs,bnskd->bnqkgd', p.astype(v.dtype), vv)
    return o.reshape(B, T, Q_SWA), k[:, -WINDOW:], v[:, -WINDOW:]


def swa_sample(q, k, v, ck, cv, sinks):
    B, T = q.shape[:2]
    kk = jnp.concatenate([ck, k], axis=1)
    vv = jnp.concatenate([cv, v], axis=1)
    qg = q.reshape(B, T, SWA_KV_HEADS, SWA_GROUP, HEAD_DIM)
    s = jnp.einsum('bqkgd,bskd->bkgqs', qg, kk).astype(jnp.float32) * (HEAD_DIM ** -0.5)
    rel = (WINDOW + jnp.arange(T))[:, None] - jnp.arange(WINDOW + T)[None, :]
    mask = (rel >= 0) & (rel <= WINDOW)
    p = sink_softmax(s, sinks.reshape(SWA_KV_HEADS, SWA_GROUP), mask)
    o = jnp.einsum('bkgqs,bskd->bqkgd', p.astype(v.dtype), vv)
    return o.reshape(B, T, Q_SWA), kk[:, -WINDOW:], vv[:, -WINDOW:]


def mem_attend(q, mk, mv):
    s = jnp.einsum('bthd,bmhd->bhtm', q, mk).astype(jnp.float32) * (HEAD_DIM ** -0.5)
    p = jax.nn.softmax(s, axis=-1)
    o = jnp.einsum('bhtm,bmhd->bthd', p.astype(mv.dtype), mv)
    return o.reshape(q.shape[0], q.shape[1], Q_MEM)


def causal_conv(xr, buf, w, b):
    T = xr.shape[1]
    xc = jnp.concatenate([buf, xr], axis=1)
    y = b + sum(xc[:, j:j + T] * w[j] for j in range(CONV_WIDTH))
    return y, xc[:, -(CONV_WIDTH - 1):]


def linear_scan(a, b, h0):
    b = b.at[:, 0].add(a[:, 0] * h0)

    def combine(left, right):
        return (left[0] * right[0], right[0] * left[1] + right[1])

    _, h = lax.associative_scan(combine, (a, b), axis=1)
    return h


def rg_lru(u, h0, wa, ba, wx, bx, lam):
    B, T, _ = u.shape
    ub = u.reshape(B, T, RNN_BLOCKS, RNN_BLOCK_DIM)
    r = jax.nn.sigmoid(jnp.einsum('btnd,nde->btne', ub, wa) + ba).reshape(B, T, RNN_WIDTH)
    i = jax.nn.sigmoid(jnp.einsum('btnd,nde->btne', ub, wx) + bx).reshape(B, T, RNN_WIDTH)
    log_a = -RG_C * r.astype(jnp.float32) * jax.nn.softplus(-lam.astype(jnp.float32))
    a = jnp.exp(log_a)
    bt = jnp.sqrt(-jnp.expm1(2.0 * log_a)) * (i * u).astype(jnp.float32)
    h = linear_scan(a, bt, h0.astype(jnp.float32))
    return h.astype(u.dtype), h[:, -1].astype(h0.dtype)


def decoder_layer(x, lp, mk, mv, conv_buf, h0, swa_cache):
    B, T, _ = x.shape
    x = x + 0.5 * rms_norm(swiglu(rms_norm(x, lp['ln_ffn1_pre']), lp['w_ffn1_in'], lp['w_ffn1_out']), lp['ln_ffn1_post'])
    xn = rms_norm(x, lp['ln_mix_pre'])
    q_swa, k_swa, v_swa, q_mem, x_rnn, g_rnn = jnp.split(xn @ lp['w_in'], IN_SPLITS, axis=-1)
    q_swa = q_swa.reshape(B, T, SWA_HEADS, HEAD_DIM)
    k_swa = k_swa.reshape(B, T, SWA_KV_HEADS, HEAD_DIM)
    v_swa = v_swa.reshape(B, T, SWA_KV_HEADS, HEAD_DIM)
    if swa_cache is None:
        o_swa, nk, nv = swa_prompt(q_swa, k_swa, v_swa, lp['swa_sinks'])
    else:
        o_swa, nk, nv = swa_sample(q_swa, k_swa, v_swa, swa_cache[0], swa_cache[1], lp['swa_sinks'])
    o_mem = mem_attend(q_mem.reshape(B, T, MEM_HEADS, HEAD_DIM), mk, mv)
    u, new_conv = causal_conv(x_rnn, conv_buf, lp['conv_w'], lp['conv_b'])
    h, h_last = rg_lru(u, h0, lp['rg_wa'], lp['rg_ba'], lp['rg_wx'], lp['rg_bx'], lp['rg_lambda'])
    o_rnn = h * jax.nn.gelu(g_rnn)
    o = jnp.concatenate([o_swa, o_mem, o_rnn], axis=-1) @ lp['w_out']
    x = x + rms_norm(o, lp['ln_mix_post'])
    x = x + 0.5 * rms_norm(swiglu(rms_norm(x, lp['ln_ffn2_pre']), lp['w_ffn2_in'], lp['w_ffn2_out']), lp['ln_ffn2_post'])
    return x, nk, nv, new_conv, h_last


def setup_inputs(seed: int = 0) -> dict:
    key = jax.random.key(seed)
    ks = jax.random.split(key, 40)
    nrm = lambda k, shape, scale=1.0: scale * jax.random.normal(k, shape, jnp.float32)
    gain = lambda k: 1.0 + nrm(k, (DEPTH, D_MODEL), 0.02)
    a_init = jax.random.uniform(ks[30], (DEPTH, RNN_WIDTH), jnp.float32, 0.9, 0.999)
    sig = a_init ** (1.0 / RG_C)
    rg_lambda = jnp.log(sig) - jnp.log1p(-sig)
    return {
        'x_prompt': nrm(ks[0], (BATCH, SEQ, D_MODEL)),
        'x_sample': nrm(ks[1], (DEC_BATCH, DEC_SEQ, D_MODEL)),
        'mem_prompt': nrm(ks[2], (BATCH, N_MEM, D_MODEL)),
        'cache_swa_k': nrm(ks[3], (DEPTH, DEC_BATCH, WINDOW, SWA_KV_HEADS, HEAD_DIM)),
        'cache_swa_v': nrm(ks[4], (DEPTH, DEC_BATCH, WINDOW, SWA_KV_HEADS, HEAD_DIM)),
        'cache_mem_k': nrm(ks[5], (DEPTH, DEC_BATCH, N_MEM, MEM_HEADS, HEAD_DIM)),
        'cache_mem_v': nrm(ks[6], (DEPTH, DEC_BATCH, N_MEM, MEM_HEADS, HEAD_DIM)),
        'state_conv': nrm(ks[7], (DEPTH, DEC_BATCH, CONV_WIDTH - 1, RNN_WIDTH)),
        'state_rglru_h': nrm(ks[8], (DEPTH, DEC_BATCH, RNN_WIDTH), 0.5),
        'ln_ffn1_pre': gain(ks[9]),
        'ln_ffn1_post': gain(ks[10]),
        'w_ffn1_in': nrm(ks[11], (DEPTH, D_MODEL, 2 * D_FF), D_MODEL ** -0.5),
        'w_ffn1_out': nrm(ks[12], (DEPTH, D_FF, D_MODEL), D_FF ** -0.5),
        'ln_mix_pre': gain(ks[13]),
        'ln_mix_post': gain(ks[14]),
        'w_in': nrm(ks[15], (DEPTH, D_MODEL, D_IN), D_MODEL ** -0.5),
        'w_out': nrm(ks[16], (DEPTH, D_MIX, D_MODEL), D_MIX ** -0.5),
        'swa_sinks': nrm(ks[17], (DEPTH, SWA_HEADS), 0.5),
        'conv_w': nrm(ks[18], (DEPTH, CONV_WIDTH, RNN_WIDTH), CONV_WIDTH ** -0.5),
        'conv_b': nrm(ks[19], (DEPTH, RNN_WIDTH), 0.01),
        'rg_wa': nrm(ks[20], (DEPTH, RNN_BLOCKS, RNN_BLOCK_DIM, RNN_BLOCK_DIM), RNN_BLOCK_DIM ** -0.5),
        'rg_ba': nrm(ks[21], (DEPTH, RNN_BLOCKS, RNN_BLOCK_DIM), 0.01),
        'rg_wx': nrm(ks[22], (DEPTH, RNN_BLOCKS, RNN_BLOCK_DIM, RNN_BLOCK_DIM), RNN_BLOCK_DIM ** -0.5),
        'rg_bx': nrm(ks[23], (DEPTH, RNN_BLOCKS, RNN_BLOCK_DIM), 0.01),
        'rg_lambda': rg_lambda,
        'ln_mem': gain(ks[24]),
        'w_mem_kv': nrm(ks[25], (DEPTH, D_MODEL, 2 * Q_MEM), D_MODEL ** -0.5),
        'ln_ffn2_pre': gain(ks[26]),
        'ln_ffn2_post': gain(ks[27]),
        'w_ffn2_in': nrm(ks[28], (DEPTH, D_MODEL, 2 * D_FF), D_MODEL ** -0.5),
        'w_ffn2_out': nrm(ks[29], (DEPTH, D_FF, D_MODEL), D_FF ** -0.5),
    }


def reference(x_prompt, x_sample, mem_prompt, cache_swa_k, cache_swa_v, cache_mem_k, cache_mem_v,
              state_conv, state_rglru_h, ln_ffn1_pre, ln_ffn1_post, w_ffn1_in, w_ffn1_out,
              ln_mix_pre, ln_mix_post, w_in, w_out, swa_sinks, conv_w, conv_b, rg_wa, rg_ba,
              rg_wx, rg_bx, rg_lambda, ln_mem, w_mem_kv, ln_ffn2_pre, ln_ffn2_post,
              w_ffn2_in, w_ffn2_out):
    B = x_prompt.shape[0]
    yp, ys = x_prompt, x_sample
    skp, svp, mkp, mvp, cvp, hp = [], [], [], [], [], []
    sks, svs, cvs, hs = [], [], [], []
    for l in range(DEPTH):
        lp = dict(ln_ffn1_pre=ln_ffn1_pre[l], ln_ffn1_post=ln_ffn1_post[l],
                  w_ffn1_in=w_ffn1_in[l], w_ffn1_out=w_ffn1_out[l],
                  ln_mix_pre=ln_mix_pre[l], ln_mix_post=ln_mix_post[l],
                  w_in=w_in[l], w_out=w_out[l], swa_sinks=swa_sinks[l],
                  conv_w=conv_w[l], conv_b=conv_b[l], rg_wa=rg_wa[l], rg_ba=rg_ba[l],
                  rg_wx=rg_wx[l], rg_bx=rg_bx[l], rg_lambda=rg_lambda[l],
                  ln_ffn2_pre=ln_ffn2_pre[l], ln_ffn2_post=ln_ffn2_post[l],
                  w_ffn2_in=w_ffn2_in[l], w_ffn2_out=w_ffn2_out[l])
        mkv = (rms_norm(mem_prompt, ln_mem[l]) @ w_mem_kv[l]).reshape(B, N_MEM, 2, MEM_HEADS, HEAD_DIM)
        mk_p, mv_p = mkv[:, :, 0], mkv[:, :, 1]
        conv0 = jnp.zeros((B, CONV_WIDTH - 1, RNN_WIDTH), x_prompt.dtype)
        h0 = jnp.zeros((B, RNN_WIDTH), x_prompt.dtype)
        yp, nk, nv, nc, nh = decoder_layer(yp, lp, mk_p, mv_p, conv0, h0, None)
        skp.append(nk); svp.append(nv); mkp.append(mk_p); mvp.append(mv_p); cvp.append(nc); hp.append(nh)
        ys, nk, nv, nc, nh = decoder_layer(ys, lp, cache_mem_k[l], cache_mem_v[l], state_conv[l],
                                           state_rglru_h[l], (cache_swa_k[l], cache_swa_v[l]))
        sks.append(nk); svs.append(nv); cvs.append(nc); hs.append(nh)
    return (yp, ys,
            jnp.stack(skp), jnp.stack(svp), jnp.stack(mkp), jnp.stack(mvp), jnp.stack(cvp), jnp.stack(hp),
            jnp.stack(sks), jnp.stack(svs), jnp.stack(cvs), jnp.stack(hs))
```

```python
import numpy as np
import concourse.bass as bass
import concourse.mybir as mybir
from concourse.bass_utils import run_bass_kernel_spmd

F32 = mybir.dt.float32
BF16 = mybir.dt.bfloat16
AF = mybir.ActivationFunctionType
ALU = mybir.AluOpType

D = 1024
T = 4096
DFF = 2816
NFC = 22
NT = 512
EPS = 1e-6
NCORES = 8
DB = 16
LAG = 2


class Ins:
    __slots__ = ("eng", "fn", "idx", "deps", "is_dma", "dma_key", "tok_val",
                 "needs_inc", "waits")

    def __init__(self, eng, fn, idx, is_dma, dma_key):
        self.eng = eng
        self.fn = fn
        self.idx = idx
        self.is_dma = is_dma
        self.dma_key = dma_key
        self.deps = []
        self.tok_val = None
        self.needs_inc = False
        self.waits = None


class Prog:
    ENGS = ("pe", "act", "dve", "pool", "sp")

    def __init__(self, nc):
        self.nc = nc
        self.lists = {e: [] for e in self.ENGS}
        self.last_w = {}
        self.readers = {}
        self.dma_count = {}
        self.final_dma = []
        self.expand = {}

    def add(self, eng, fn, reads=(), writes=(), dma_key=None, final=False):
        is_dma = dma_key is not None
        ins = Ins(eng, fn, len(self.lists[eng]), is_dma, dma_key)
        deps = {}
        if self.expand:
            reads = [x for r in reads for x in self.expand.get(r, (r,))]
            writes = [x for w in writes for x in self.expand.get(w, (w,))]
        psum_reads = [r for r in reads if isinstance(r, tuple) and r[0] in ("ps", "psb")] if eng != "pe" else []

        def dep(d, kind):
            if d is ins:
                return
            k = id(d)
            if k in deps:
                if kind != "war":
                    deps[k] = (d, kind)
            else:
                deps[k] = (d, kind)

        for r in reads:
            lw = self.last_w.get(r)
            if lw is not None:
                dep(lw, "raw")
        for r in psum_reads:
            rd = self.readers.get(r)
            if rd:
                for d in rd["eng"].values():
                    if d.eng != eng:
                        dep(d, "raw")
        for w in writes:
            lw = self.last_w.get(w)
            if lw is not None:
                dep(lw, "waw")
            rd = self.readers.get(w)
            if rd:
                for d in rd["eng"].values():
                    dep(d, "war")
                for d in rd["dma"]:
                    dep(d, "war")
        for r in reads:
            rd = self.readers.setdefault(r, {"eng": {}, "dma": []})
            if is_dma:
                rd["dma"].append(ins)
            else:
                rd["eng"][eng] = ins
        for w in writes:
            self.last_w[w] = ins
            self.readers[w] = {"eng": {}, "dma": []}
        ins.deps = list(deps.values())
        if is_dma:
            c = self.dma_count.get(dma_key, 0) + 1
            self.dma_count[dma_key] = c
            ins.tok_val = 16 * c
            if final:
                self.final_dma.append(ins)
        self.lists[eng].append(ins)
        return ins

    def finalize(self):
        for e in self.ENGS:
            waited_idx = {}
            waited_dma = {}
            for ins in self.lists[e]:
                w_eng = {}
                w_dma = {}
                for d, kind in ins.deps:
                    if d.is_dma:
                        if waited_dma.get(d.dma_key, 0) >= d.tok_val:
                            continue
                        if w_dma.get(d.dma_key, 0) < d.tok_val:
                            w_dma[d.dma_key] = d.tok_val
                    else:
                        if d.eng == e and not ins.is_dma:
                            if e == "pe":
                                continue
                        if waited_idx.get(d.eng, -1) >= d.idx:
                            continue
                        cur = w_eng.get(d.eng)
                        if cur is None or cur.idx < d.idx:
                            w_eng[d.eng] = d
                for k, v in w_dma.items():
                    waited_dma[k] = v
                for se, d in w_eng.items():
                    waited_idx[se] = d.idx
                    d.needs_inc = True
                ins.waits = (w_eng, w_dma)
        for e in self.ENGS:
            c = 0
            for ins in self.lists[e]:
                if ins.is_dma:
                    continue
                if ins.needs_inc:
                    c += 1
                    ins.tok_val = c

    def emit_engine(self, e, handle, sems, dma_sems):
        for ins in self.lists[e]:
            w_eng, w_dma = ins.waits
            for se, d in w_eng.items():
                handle.wait_ge(sems[se], d.tok_val)
            for k, v in w_dma.items():
                handle.wait_ge(dma_sems[k], v)
            r = ins.fn(handle)
            if ins.is_dma:
                r.then_inc(dma_sems[ins.dma_key], 16)
            elif ins.needs_inc:
                r.then_inc(sems[e], 1)
        if e == "sp":
            done = set()
            for ins in self.final_dma:
                if ins.dma_key in done:
                    continue
                done.add(ins.dma_key)
                handle.wait_ge(dma_sems[ins.dma_key], self.dma_count[ins.dma_key] * 16)

    def run(self):
        nc = self.nc
        sems = {e: nc.alloc_semaphore("sem_" + e) for e in self.ENGS}
        dma_sems = {k: nc.alloc_semaphore("dsem_%d" % i) for i, k in enumerate(self.dma_count)}
        self.finalize()
        with nc.Block() as block:
            @block.tensor
            def _(h):
                self.emit_engine("pe", h, sems, dma_sems)

            @block.scalar
            def _(h):
                self.emit_engine("act", h, sems, dma_sems)

            @block.vector
            def _(h):
                self.emit_engine("dve", h, sems, dma_sems)

            @block.gpsimd
            def _(h):
                self.emit_engine("pool", h, sems, dma_sems)

            @block.sync
            def _(h):
                self.emit_engine("sp", h, sems, dma_sems)


class Ring:
    def __init__(self, name, items, off=0):
        self.name = name
        self.items = items
        self.i = 0
        self.off = off

    def next(self):
        k = self.i % len(self.items)
        self.i += 1
        return self.items[k], (self.name, k + self.off)


def build(n_ptiles=8, do_sample=True, stop_after=None):
    nc = bass.Bass("TRN2", target_bir_lowering=False)

    def din(name, shape):
        return nc.dram_tensor(name, shape, F32, kind="ExternalInput").ap()

    def dout(name, shape):
        return nc.dram_tensor(name, shape, F32, kind="ExternalOutput").ap()

    x_prompt = din("x_prompt", [T, D])
    x_sample = din("x_sample", [DB, D])
    mem_prompt = din("mem_prompt", [256, D])
    cache_swa_k = din("cache_swa_k", [DB, 128, 128])
    cache_swa_v = din("cache_swa_v", [DB, 128, 128])
    cache_mem_k = din("cache_mem_k", [DB, 256, 256])
    cache_mem_v = din("cache_mem_v", [DB, 256, 256])
    state_conv = din("state_conv", [DB, 3, 256])
    state_h = din("state_rglru_h", [DB, 256])
    ln_pre = [din("ln_ffn1_pre", [D]), din("ln_mix_pre", [D]), din("ln_ffn2_pre", [D]), din("ln_mem", [D])]
    ln_post = [din("ln_ffn1_post", [D]), din("ln_mix_post", [D]), din("ln_ffn2_post", [D])]
    w_ffn_in = [din("w_ffn1_in", [D, 2 * DFF]), din("w_ffn2_in", [D, 2 * DFF])]
    w_ffn_out = [din("w_ffn1_out", [DFF, D]), din("w_ffn2_out", [DFF, D])]
    w_in = din("w_in", [D, 1536])
    w_out = din("w_out", [D, D])
    swa_sinks = din("swa_sinks", [8])
    conv_w = din("conv_w", [4, 256])
    conv_b = din("conv_b", [256])
    rg_wa = din("rg_wa", [4, 64, 64])
    rg_ba = din("rg_ba", [256])
    rg_wx = din("rg_wx", [4, 64, 64])
    rg_bx = din("rg_bx", [256])
    rg_lambda = din("rg_lambda", [256])
    w_mem_kv = din("w_mem_kv", [D, 512])

    y_prompt = dout("y_prompt", [T, D])
    y_sample = dout("y_sample", [DB, D])
    o_swa_k_p = dout("swa_k_p", [128, 128])
    o_swa_v_p = dout("swa_v_p", [128, 128])
    o_mem_k_p = dout("mem_k_p", [256, 256])
    o_mem_v_p = dout("mem_v_p", [256, 256])
    o_conv_p = dout("conv_p", [3, 256])
    o_h_p = dout("h_p", [256])
    o_swa_k_s = dout("swa_k_s", [DB, 128, 128])
    o_swa_v_s = dout("swa_v_s", [DB, 128, 128])
    o_conv_s = dout("conv_s", [DB, 3, 256])
    o_h_s = dout("h_s", [DB, 256])

    P = Prog(nc)

    def sb(name, shape, dt=F32):
        return nc.alloc_sbuf_tensor(name, shape, dt)

    stage = [sb("stage%d" % i, [128, 4096], F32) for i in range(2)]
    stage_ring = Ring("stage", stage)
    wslab = [sb("wslab%d" % i, [128, 8, 512], BF16) for i in range(2)]
    wslab_ring = Ring("wslab", wslab)
    wsr = [wslab_ring]
    claimed = set()

    def wslab_claim(wsn):
        if wsn[0] == "wslabx" and wsn not in claimed:
            claimed.add(wsn)
            return [("stage", wsn[1] // 2)]
        return []
    wbig = sb("wbig", [128, NFC * 1024], BF16)
    x_res = sb("x_res", [128, 4, D], F32)
    xn_ring = Ring("xn", [sb("xn%d" % i, [128, D], BF16) for i in range(2)])
    xnT = sb("xnT", [128, 8, NT], BF16)
    gT = sb("gT", [128, NFC, NT], BF16)
    tmp_ring = Ring("tmp", [sb("tmp%d" % i, [128, D], F32) for i in range(1)])
    x_s = sb("x_s", [DB, D], F32)
    xnT_s = sb("xnT_s", [128, 8, DB], BF16)
    gT_s = sb("gT_s", [128, NFC, DB], BF16)
    gpost = [sb("gpost%d" % i, [128, D], F32) for i in range(3)]
    gpre = sb("gpre", [128, 4, 8], F32)
    junk = sb("junk", [128, D], BF16)
    small = sb("small", [128, 64], F32)
    small_i = [0]

    def small_next(tag):
        k = small_i[0] % 64
        small_i[0] += 1
        return small[:, k:k + 1], ("small", k)

    ident = sb("ident", [128, 128], BF16)
    neg_half = sb("neg_half", [128, 1], F32)
    eps_c = sb("eps_c", [128, 2], F32)
    gflat = gT[:].rearrange("p a b -> p (a b)")
    qT = gflat[:, 0:2048].rearrange("p (a b) -> p a b", a=4)
    kdupT = gflat[:, 2048:2048 + 1280].rearrange("p (a b) -> p a b", a=2)
    qmT = gflat[:, 3584:3584 + 1024].rearrange("p (a b) -> p a b", a=2)
    omixT = gflat[:, 4608:4608 + 4096].rearrange("p (a b) -> p a b", a=8)
    pt_ring = Ring("gT", [gflat[:, (17 + i) * 512: (18 + i) * 512] for i in range(5)], off=17)
    QN = lambda c: ("gT", c)
    KDN = [("gT", 4), ("gT", 5), ("gT", 6)]
    QMN = lambda c: ("gT", 7 + c)
    OMN = lambda i: ("gT", 9 + i)
    kd_hist = sb("kd_hist", [128, 2, 128], BF16)
    vaug = sb("vaug", [128, 5, 2, 2, 128], BF16)
    xr = sb("xr", [128, 2, 3 + NT], F32)
    hbuf = sb("hbuf", [128, 2, 1 + NT], F32)
    rgbuf = sb("rgbuf", [128, 2, NT], F32)
    rg_u = rgbuf[:, 0, :]
    rg_r = rgbuf[:, 1, :]
    memkv = rgbuf
    MEMKVN = ["rg_u", "rg_r"]
    rg_ub = sb("rg_ub", [128, NT], BF16)
    rg_i = sb("rg_i", [128, NT], F32)
    rg_a = sb("rg_a", [128, NT], F32)
    rg_g = sb("rg_g", [128, 2, NT], F32)
    rd_ring = Ring("rd", [sb("rd%d" % i, [128, NT], F32) for i in range(2)])
    mask_po = sb("mask_po", [128, 512], BF16)
    sink_full = sb("sink_full", [128, 1024], BF16)
    e_sel = sb("e_sel", [128, 2, 128], BF16)
    sink_raw = sb("sink_raw", [1, 8], F32)
    sink_exp = sb("sink_exp", [1, 8], F32)
    convw = sb("convw", [128, 2, 4], F32)
    rgc = sb("rgc", [128, 8, 2], F32)
    wbd = sb("wbd", [128, 2, 2, 128], BF16)
    mkT = sb("mkT", [128, 2, 256], BF16)
    mvaug = sb("mvaug", [128, 2, 4, 128], BF16)
    kvout = sb("kvout", [128, 256], F32)
    vxaug = sb("vxaug", [1, 2, 512], BF16)

    ps = nc.alloc_psum_tensor("ps", [128, 6, 512], F32)
    psb = nc.alloc_psum_tensor("psb", [128, 2, 1024], BF16)
    bank_i = [0]
    pair_i = [0]
    psb_i = [0]

    pending_banks = set()

    def bank():
        while True:
            k = bank_i[0] % 6
            bank_i[0] += 1
            if k not in pending_banks:
                return k

    def pair():
        k = (pair_i[0] % 3) * 2
        pair_i[0] += 1
        return k

    def psb_next():
        k = psb_i[0] % 2
        psb_i[0] += 1
        return k

    PSN = lambda k: ("ps", k)

    def dma_in(dst, src, names, key):
        P.add("sp", lambda h: h.dma_start(out=dst, in_=src), writes=names, dma_key=key)

    def dma_in_nc(dst, src, names, key):
        P.add("sp", lambda h: h.dma_start(out=dst, in_=src, allow_slow_non_contiguous=True), writes=names, dma_key=key)

    def dma_cst(dst, src, names, nonc=False):
        if nonc:
            P.add("act", lambda h: h.dma_start(out=dst, in_=src, allow_slow_non_contiguous=True), writes=list(names) + ["cst_chain"], dma_key="cst")
        else:
            P.add("act", lambda h: h.dma_start(out=dst, in_=src), writes=list(names) + ["cst_chain"], dma_key="cst")

    def dma_out(dst, src, names, key, nonc=False):
        if nonc:
            P.add("sp", lambda h: h.dma_start(out=dst, in_=src, allow_slow_non_contiguous=True), reads=names, dma_key=key, final=True)
        else:
            P.add("sp", lambda h: h.dma_start(out=dst, in_=src), reads=names, dma_key=key, final=True)

    def mm(out, lhsT, rhs, start, stop, reads, writes, sgc=False):
        if sgc:
            P.add("pe", lambda h: h.matmul(out, lhsT=lhsT, rhs=rhs, start=start, stop=stop, skip_group_check=True), reads=reads, writes=writes)
        else:
            P.add("pe", lambda h: h.matmul(out, lhsT=lhsT, rhs=rhs, start=start, stop=stop), reads=reads, writes=writes)

    def act(out, in_, func, reads, writes, scale=None, bias=None, accum=None):
        kw = {}
        if scale is not None:
            kw["scale"] = scale
        if bias is not None:
            kw["bias"] = bias
        if accum is not None:
            kw["accum_out"] = accum
        P.add("act", lambda h: h.activation(out=out, in_=in_, func=func, **kw), reads=reads, writes=writes)

    def tt(eng, out, in0, in1, op, reads, writes):
        P.add(eng, lambda h: h.tensor_tensor(out=out, in0=in0, in1=in1, op=op), reads=reads, writes=writes)

    def ts(eng, out, in0, s1, s2, op0, op1, reads, writes):
        if op1 is None:
            P.add(eng, lambda h: h.tensor_scalar(out=out, in0=in0, scalar1=s1, scalar2=None, op0=op0), reads=reads, writes=writes)
        else:
            P.add(eng, lambda h: h.tensor_scalar(out=out, in0=in0, scalar1=s1, scalar2=s2, op0=op0, op1=op1), reads=reads, writes=writes)

    def stt(out, in0, scalar, in1, op0, op1, reads, writes):
        P.add("dve", lambda h: h.scalar_tensor_tensor(out=out, in0=in0, scalar=scalar, in1=in1, op0=op0, op1=op1), reads=reads, writes=writes)

    def copy(eng, out, in_, reads, writes):
        P.add(eng, lambda h: h.tensor_copy(out=out, in_=in_), reads=reads, writes=writes)

    def memset(eng, ap, val, writes):
        P.add(eng, lambda h: h.memset(ap, val), writes=writes)

    def recip(out, in_, reads, writes):
        P.add("dve", lambda h: h.reciprocal(out=out, in_=in_), reads=reads, writes=writes)

    cst_i = [0]

    def cst_key():
        cst_i[0] += 1
        return "cst%d" % cst_i[0]

    ones_f = stage[1][:, 0:512]
    memset("pool", ones_f, 1.0, ["ones_f", ("stage", 1)])
    memset("pool", neg_half[:], -0.5, ["neg_half"])
    memset("pool", eps_c[:, 0:1], EPS, ["eps_c"])
    memset("pool", eps_c[:, 1:2], 4.0 * EPS, ["eps_c"])
    P.add("pool", lambda h: h.affine_select(out=ident[:], in_=ones_f[:, 0:128], pattern=[[-1, 128]],
                                            compare_op=ALU.is_equal, fill=0.0, base=0, channel_multiplier=1),
          reads=["ones_f", ("stage", 1)], writes=["ident"])
    zeros_f = stage[1][:, 512:768]
    memset("pool", zeros_f, 0.0, ["zeros_f", ("stage", 1)])
    P.add("pool", lambda h: h.affine_select(out=mask_po[:, 0:256], in_=zeros_f, pattern=[[0, 2], [-1, 128]],
                                            compare_op=ALU.is_ge, fill=-30000.0, base=0, channel_multiplier=1),
          reads=["zeros_f", ("stage", 1)], writes=["mask_po"])
    P.add("pool", lambda h: h.affine_select(out=mask_po[:, 256:512], in_=zeros_f, pattern=[[0, 2], [1, 128]],
                                            compare_op=ALU.is_ge, fill=-30000.0, base=0, channel_multiplier=-1),
          reads=["zeros_f", ("stage", 1), "mask_po"], writes=["mask_po"])
    for i in range(4):
        dma_cst(gpre[:, i, :], ln_pre[i].rearrange("(k p) -> p k", p=128), [("gpre", i)], nonc=True)
    for i in range(3):
        dma_cst(gpost[i][:], ln_post[i].partition_broadcast(128), [("gpost", i)])
    for c in range(2):
        dma_cst(convw[:, c, :], conv_w[:, c * 128:(c + 1) * 128].rearrange("j p -> p j"), [("convw", c)], nonc=True)
    for i, src in enumerate([conv_b, rg_ba, rg_bx, rg_lambda]):
        dma_cst(rgc[:, i, :], src.rearrange("(c p) -> p c", p=128), [("rgc", i)], nonc=True)
    act(rgc[:, 6, :], rgc[:, 3, :], AF.Exp, [("rgc", 3)], [("rgc", 6)], scale=-1.0)
    act(rgc[:, 7, :], rgc[:, 6, :], AF.Ln, [("rgc", 6)], [("rgc", 7)], bias=1.0)
    ts("dve", rgc[:, 4, :], rgc[:, 7, :], -8.0, None, ALU.mult, None, [("rgc", 7)], [("rgc", 4)])
    ts("dve", rgc[:, 5, :], rgc[:, 7, :], -16.0, None, ALU.mult, None, [("rgc", 7)], [("rgc", 5)])
    memset("pool", sink_full[:], 0.0, ["sink_full"])
    dma_cst(sink_raw[:], swa_sinks.rearrange("(a h) -> a h", a=1), ["sink_raw"])
    act(sink_exp[:], sink_raw[:], AF.Exp, ["sink_raw"], ["sink_exp"])
    for kv in range(2):
        src = bass.AP(sink_exp, 4 * kv, [[8, 1], [1, 2], [2, 2], [0, 128]])
        dst = sink_full[0:1, kv * 512:(kv + 1) * 512].rearrange("o (p j q) -> o p j q", p=2, j=2)
        copy("dve", dst, src, ["sink_exp", "sink_full"], ["sink_full"])
    memset("pool", e_sel[:], 0.0, ["e_sel"])
    memset("pool", e_sel[0:1, 0, 64:128], 1.0, ["e_sel"])
    memset("pool", e_sel[0:1, 1, 0:64], 1.0, ["e_sel"])
    memset("pool", vaug[:], 1.0, ["vaug_init"])
    memset("pool", mvaug[:], 1.0, ["mvaug"])
    memset("pool", xr[:, :, 0:3], 0.0, ["xr_hist"])
    memset("pool", hbuf[:, :, 0:1], 0.0, ["h_carry"])
    memset("pool", kd_hist[:], 0.0, ["kd_hist"])
    st0 = stage[0]
    stv = st0[:, 0:512].rearrange("p (g c e) -> p g c e", g=2, c=2)
    memset("pool", st0[:, 0:512], 0.0, [("stage", 0)])
    for gi, wsrc in enumerate([rg_wa, rg_wx]):
        for c in range(2):
            for l in range(2):
                dma_cst(stv[l * 64:(l + 1) * 64, gi, c, l * 64:(l + 1) * 64], wsrc[2 * c + l], [("stage", 0)])
    copy("pool", wbd[:], stv, [("stage", 0)], ["wbd"])
    stage_ring.i = 1

    def load_piece(src_ap, a, b, dst_ap, dst_names, reads_extra=()):
        st, st_name = stage_ring.next()
        stv_ = st[:, 0:a * b].rearrange("p (a b) -> p a b", a=a)
        P.add("sp", lambda h: h.dma_start(out=stv_, in_=src_ap), writes=[st_name], dma_key=st_name)
        P.add("pool", lambda h: h.tensor_copy(out=dst_ap, in_=stv_), reads=[st_name] + list(reads_extra), writes=dst_names)

    scratch = {}
    pending = []
    cast_i = [0]
    store_i = [0]
    CAST_ENGS = ["act", "dve", "pool", "act", "dve"]

    def cast_op(dst_ap, src_ap, reads, writes):
        e = CAST_ENGS[cast_i[0] % len(CAST_ENGS)]
        cast_i[0] += 1
        if e == "act":
            act(dst_ap, src_ap, AF.Copy, reads, writes)
        else:
            copy(e, dst_ap, src_ap, reads, writes)

    def flush_stores(keep=0):
        while len(pending) > keep:
            pid, dst_ap, dst_names = pending.pop(0)
            k = store_i[0] % 6
            store_i[0] += 1
            P.add("sp", (lambda pid, dst_ap: lambda h: h.dma_start(out=scratch[pid], in_=dst_ap))(pid, dst_ap),
                  reads=dst_names, writes=[("scr", pid), ("scrkey", k)], dma_key=("scrkey", k))

    def cached_load(pid, dst_ap, dst_names):
        P.add("sp", lambda h: h.dma_start(out=dst_ap, in_=scratch[pid]), reads=[("scr", pid)], writes=dst_names,
              dma_key=("ld",) + tuple(dst_names[0]))

    def piece(pid, src_ap, a, b, dst_ap, dst_names, post_cast=None, first_names=None):
        if pid not in scratch:
            if first_names is not None:
                dst_names = first_names
            scratch[pid] = nc.dram_tensor("scr_" + pid, [128, a, b], BF16).ap()
            st, st_name = stage_ring.next()
            stv_ = st[:, 0:a * b].rearrange("p (a b) -> p a b", a=a)
            P.add("sp", lambda h: h.dma_start(out=stv_, in_=src_ap), writes=[st_name], dma_key=st_name)
            cast_op(dst_ap, stv_, [st_name], dst_names)
            if post_cast is not None:
                post_cast(stv_, st_name)
            pending.append((pid, dst_ap, dst_names))
            flush_stores(keep=1)
        else:
            cached_load(pid, dst_ap, dst_names)

    def rstd_from_ss(ss_ap, ss_name, c1, c2, pn):
        v, vn = small_next("v")
        r, rn = small_next("r")
        ts("dve", v[0:pn], ss_ap[0:pn], c1, c2, ALU.mult, ALU.add, [ss_name], [vn]) if False else None
        act(v[0:pn], ss_ap[0:pn], AF.Ln, [ss_name, "eps_c"], [vn], scale=c1, bias=(eps_c[0:pn, 0:1] if c2 == EPS else eps_c[0:pn, 1:2]))
        act(r[0:pn], v[0:pn], AF.Exp, [vn], [rn], scale=-0.5)
        return r, rn

    def prenorm_a(src, src_name, pn):
        ss, ssn = small_next("ss")
        act(junk[0:pn], src, AF.Square, [src_name], [ssn, "junk"], accum=ss[0:pn])
        r, rn = rstd_from_ss(ss, ssn, 1.0 / D, EPS, pn)
        xn, xnn = xn_ring.next()
        act(xn[0:pn], src, AF.Copy, [src_name, rn], [xnn], scale=r[0:pn])
        return xn, xnn

    def prenorm_sub(s, src, src_name, pn, gi, dstT, dst_name, pre=None):
        xn, xnn = pre if pre is not None else prenorm_a(src, src_name, pn)
        k = psb_next()
        for kc in range(8):
            P.add("pe", (lambda k, kc, xn: lambda h: h.transpose(out=psb[:, k, kc * 128:kc * 128 + pn], in_=xn[0:pn, kc * 128:(kc + 1) * 128],
                                                               identity=ident[0:pn, 0:pn]))(k, kc, xn),
                  reads=[xnn, "ident"], writes=[("psb", k)])
        fine = tuple(("xk", dst_name, kc) for kc in range(8))
        P.expand[dst_name] = fine
        for kc in range(8):
            ts("dve", dstT[:, kc, s * 128:s * 128 + pn], psb[:, k, kc * 128:kc * 128 + pn], gpre[:, gi, kc:kc + 1], None, ALU.mult, None,
               [("psb", k), ("gpre", gi)], [fine[kc]])

    def prenorm_transpose(src_sub, src_names, nsub, pn, gi, dstT, dst_name_fn):
        for s in range(nsub):
            prenorm_sub(s, src_sub(s), src_names[s], pn, gi, dstT, dst_name_fn(s))

    deferred = []

    def run_deferred():
        while deferred:
            deferred.pop(0)()

    def out_loop(nsub, pn, group, gi_post, half_factor, after_sub):
        for s in range(nsub):
            pk = group(s)
            postnorm_residual(pk, s, pn, gi_post, half_factor)
            if after_sub is not None:
                if s >= 2:
                    after_sub[1](s - 2)
                after_sub[0](s)
        if after_sub is not None:
            for s2 in range(max(0, nsub - 2), nsub - 1):
                after_sub[1](s2)
            deferred.append((lambda f, s_: lambda: f(s_))(after_sub[1], nsub - 1))

    def postnorm_residual(pk, s, pn, gi, half_factor, xbuf=None, xname=None):
        if xbuf is None:
            xbuf, xname = x_res[0:pn, s, :], ("x", s)
        y = ps[0:pn, pk:pk + 2, :]
        ss, ssn = small_next("ss")
        act(junk[0:pn].rearrange("p (a b) -> p a b", a=2), y, AF.Square, [PSN(pk), PSN(pk + 1)], [ssn, "junk"], accum=ss[0:pn])
        tmp, tmpn = tmp_ring.next()
        tt("dve", tmp[0:pn].rearrange("p (a b) -> p a b", a=2), y, gpost[gi][0:pn].rearrange("p (a b) -> p a b", a=2), ALU.mult,
           [PSN(pk), PSN(pk + 1), ("gpost", gi)], [tmpn])
        if half_factor:
            r, rn = rstd_from_ss(ss, ssn, 4.0 / D, 4.0 * EPS, pn)
        else:
            r, rn = rstd_from_ss(ss, ssn, 1.0 / D, EPS, pn)
        stt(xbuf, tmp[0:pn], r[0:pn], xbuf, ALU.mult, ALU.add, [tmpn, rn, xname], [xname])

    prefetched = {}
    cached_mode = [False]

    def slab_src(fi, j, part):
        c0 = j * 512
        w = min(512, DFF - c0)
        return w_ffn_in[fi][:, part * DFF + c0: part * DFF + c0 + w].rearrange("(k p) f -> p k f", p=128), w

    def load_slab(fi, j, part):
        src, w = slab_src(fi, j, part)
        wsl, wsn = wsr[0].next()
        piece("f%ds%dp%d" % (fi, j, part), src, 8, w, wsl[:, :, 0:w], [wsn] + wslab_claim(wsn))
        return wsl, wsn

    def prefetch_slabs(fi, count):
        order = [(j, part) for j in range(6) for part in range(2)]
        for (j, part) in order[:count]:
            prefetched[(fi, j, part)] = load_slab(fi, j, part)

    def ffn(fi, gi_pre, gi_post, n, nsub, pn, do_pre=True, after_sub=None, before_out=None, ride=False):
        if ride:
            prenorm_sub(0, x_s[:], "x_s", DB, gi_pre, xnT_s, "xnT_s")
        if do_pre:
            prenorm_transpose(lambda s: x_res[0:pn, s, :], [("x", s) for s in range(nsub)], nsub, pn, gi_pre, xnT,
                              lambda s: ("xnT", s))
        xnT_names = [("xnT", s) for s in range(nsub)]
        wout = w_ffn_out[fi]
        split = cached_mode[0] and n == NT and len(deferred) > 0
        if split:
            slabs0 = {}
            for part in range(2):
                slabs0[part] = prefetched.pop((fi, 0, part)) if (fi, 0, part) in prefetched else load_slab(fi, 0, part)
            for (c0, c1, names) in ((0, 384, xnT_names[0:3]), (384, 512, xnT_names[3:4])):
                if c0 == 384:
                    run_deferred()
                for part in range(2):
                    wsl, wsn = slabs0[part]
                    for fl in range(4):
                        fc = fl
                        b = bank()
                        for kc in range(8):
                            mm(ps[:, b, 0:c1 - c0], wsl[:, kc, fl * 128:(fl + 1) * 128], xnT[:, kc, c0:c1], kc == 0, kc == 7, [wsn] + names, [PSN(b)])
                        if part == 0:
                            act(gT[:, fc, c0:c1], ps[:, b, 0:c1 - c0], AF.Silu, [PSN(b)], [("gT", fc)])
                        else:
                            tt("dve", gT[:, fc, c0:c1], ps[:, b, 0:c1 - c0], gT[:, fc, c0:c1], ALU.mult, [PSN(b), ("gT", fc)], [("gT", fc)])
                        if ride and c0 == 384:
                            b = bank()
                            for kc in range(8):
                                mm(ps[:, b, 0:DB], wsl[:, kc, fl * 128:(fl + 1) * 128], xnT_s[:, kc, :], kc == 0, kc == 7, [wsn, "xnT_s"], [PSN(b)])
                            if part == 0:
                                act(gT_s[:, fc, :], ps[:, b, 0:DB], AF.Silu, [PSN(b)], [("gT_s", fc)])
                            else:
                                tt("dve", gT_s[:, fc, :], ps[:, b, 0:DB], gT_s[:, fc, :], ALU.mult, [PSN(b), ("gT_s", fc)], [("gT_s", fc)])
        else:
            run_deferred()
        for j in range(6):
            if split and j == 0:
                continue
            w = min(512, DFF - j * 512)
            nch = w // 128
            for part in range(2):
                if (fi, j, part) in prefetched:
                    wsl, wsn = prefetched.pop((fi, j, part))
                else:
                    wsl, wsn = load_slab(fi, j, part)
                for fl in range(nch):
                    fc = j * 4 + fl
                    b = bank()
                    for kc in range(8):
                        mm(ps[:, b, 0:n], wsl[:, kc, fl * 128:(fl + 1) * 128], xnT[:, kc, 0:n], kc == 0, kc == 7,
                           [wsn] + xnT_names, [PSN(b)])
                    if part == 0:
                        act(gT[:, fc, 0:n], ps[:, b, 0:n], AF.Silu, [PSN(b)], [("gT", fc)])
                    else:
                        tt("dve", gT[:, fc, 0:n], ps[:, b, 0:n], gT[:, fc, 0:n], ALU.mult, [PSN(b), ("gT", fc)], [("gT", fc)])
                    if ride:
                        b = bank()
                        for kc in range(8):
                            mm(ps[:, b, 0:DB], wsl[:, kc, fl * 128:(fl + 1) * 128], xnT_s[:, kc, :], kc == 0, kc == 7, [wsn, "xnT_s"], [PSN(b)])
                        if part == 0:
                            act(gT_s[:, fc, :], ps[:, b, 0:DB], AF.Silu, [PSN(b)], [("gT_s", fc)])
                        else:
                            tt("dve", gT_s[:, fc, :], ps[:, b, 0:DB], gT_s[:, fc, :], ALU.mult, [PSN(b), ("gT_s", fc)], [("gT_s", fc)])
        wv = wbig[:].rearrange("p (f d) -> p f d", f=NFC)
        for pc in range(6):
            f0 = pc * 4
            nf = min(4, NFC - f0)
            src = wout[f0 * 128:(f0 + nf) * 128, :].rearrange("(f p) d -> p f d", p=128)
            piece("f%do%d" % (fi, pc), src, nf, D, wv[:, f0:f0 + nf, :], [("wbig", f0 + i) for i in range(nf)])
        if before_out is not None:
            before_out()

        def group(s):
            pk = pair()
            for half in range(2):
                for fc in range(NFC):
                    mm(ps[0:pn, pk + half, :], gT[:, fc, s * 128:s * 128 + pn], wv[:, fc, half * 512:(half + 1) * 512],
                       fc == 0, fc == NFC - 1, [("gT", fc), ("wbig", fc)], [PSN(pk + half)])
            return pk
        out_loop(nsub, pn, group, gi_post, True, after_sub)
        if ride:
            pk = pair()
            for half in range(2):
                for fc in range(NFC):
                    mm(ps[0:DB, pk + half, :], gT_s[:, fc, :], wv[:, fc, half * 512:(half + 1) * 512],
                       fc == 0, fc == NFC - 1, [("gT_s", fc), ("wbig", fc)], [PSN(pk + half)])
            postnorm_residual(pk, 0, DB, gi_post, True, xbuf=x_s[:], xname="x_s")

    WI = 1536 + 256
    wmi = wbig[:, 0:8 * WI].rearrange("p (k f) -> p k f", k=8)
    wmo = wbig[:, 8 * WI:8 * WI + 8 * D].rearrange("p (k f) -> p k f", k=8)
    ALLBIG = [("wbig", i) for i in range(NFC)]

    REG_WMI = [("wbig", f) for f in range(14)]
    REG_WMO = [("wbig", f) for f in range(14, NFC)]

    def load_mix_weights():
        KD_NAMES = [("wmi", 3 + i) for i in range(4)]
        first = "wkd" not in scratch
        for pc in range(3):
            src = w_in[:, pc * 512:(pc + 1) * 512].rearrange("(k p) f -> p k f", p=128)
            post = None
            if pc == 1 and first:
                def post(stv_, st_name):
                    for kv in range(2):
                        for dpl in range(2):
                            cast_op(wmi[:, :, 1536 + kv * 128 + dpl * 64: 1536 + kv * 128 + (dpl + 1) * 64], stv_[:, :, kv * 64:(kv + 1) * 64],
                                    [st_name], [("wmi", 3 + kv * 2 + dpl)] + REG_WMI)
            piece("wmi%d" % pc, src, 8, 512, wmi[:, :, pc * 512:(pc + 1) * 512], ALLBIG if pc == 0 else [("wmi", pc)], post_cast=post,
                  first_names=(ALLBIG if pc == 0 else [("wmi", pc)] + REG_WMI))
            if pc == 1:
                if first:
                    scratch["wkd"] = nc.dram_tensor("scr_wkd", [128, 8, 256], BF16).ap()
                    pending.append(("wkd", wmi[:, :, 1536:1792], KD_NAMES + REG_WMI))
                else:
                    cached_load("wkd", wmi[:, :, 1536:1792], KD_NAMES)
        for pc in range(2):
            src = w_out[:, pc * 512:(pc + 1) * 512].rearrange("(k p) f -> p k f", p=128)
            piece("wmo%d" % pc, src, 8, 512, wmo[:, :, pc * 512:(pc + 1) * 512], [("wmo", pc)], first_names=[("wmo", pc)] + REG_WMO)
    WMI_ALL = ALLBIG + [("wmi", i) for i in range(1, 7)]

    def mem_kv_prompt():
        mtile = [x_res[:, 0, :], x_res[:, 1, :]]
        for s in range(2):
            dma_in(mtile[s], mem_prompt[s * 128:(s + 1) * 128, :], [("x", s)], "memld%d" % s)
        prenorm_transpose(lambda s: mtile[s], [("x", 0), ("x", 1)], 2, 128, 3, xnT, lambda s: ("xnT", s))
        wsl, wsn = wsr[0].next()
        load_piece(w_mem_kv.rearrange("(k p) f -> p k f", p=128), 8, 512, wsl[:], [wsn])
        names = [("xnT", 0), ("xnT", 1), wsn]
        for s in range(2):
            b = bank()
            for kc in range(8):
                mm(ps[:, b, :], xnT[:, kc, s * 128:(s + 1) * 128], wsl[:, kc, :], kc == 0, kc == 7, names, [PSN(b)])
            act(memkv[:, s, :], ps[:, b, :], AF.Copy, [PSN(b)], [MEMKVN[s]])
            dma_out(o_mem_k_p[s * 128:(s + 1) * 128, :], memkv[:, s, 0:256], [MEMKVN[s]], "st_memk%d" % s)
            dma_out(o_mem_v_p[s * 128:(s + 1) * 128, :], memkv[:, s, 256:512], [MEMKVN[s]], "st_memv%d" % s)
            for mh in range(4):
                pm = mh % 2
                cols = slice(0, 64) if pm == 0 else slice(64, 128)
                copy("dve", mvaug[:, s, mh, cols], memkv[:, s, 256 + mh * 64: 256 + (mh + 1) * 64], [MEMKVN[s], "mvaug"], ["mvaug"])
        for cm in range(2):
            b = bank()
            for kc in range(8):
                mm(ps[:, b, 0:256], wsl[:, kc, cm * 128:(cm + 1) * 128], xnT[:, kc, 0:256], kc == 0, kc == 7, names, [PSN(b)])
            act(mkT[:, cm, :], ps[:, b, 0:256], AF.Copy, [PSN(b)], ["mkT"])

    def mixer_prompt(ti, last, do_pre=True, after_sub=None, at_start=None):
        run_deferred()
        n = NT
        if do_pre:
            prenorm_transpose(lambda s: x_res[:, s, :], [("x", s) for s in range(4)], 4, 128, 1, xnT, lambda s: ("xnT", s))
        xnT_names = [("xnT", s) for s in range(4)]
        load_mix_weights()
        if at_start is not None:
            at_start()

        def proj(col0, evac):
            b = bank()
            for kc in range(8):
                mm(ps[:, b, 0:n], wmi[:, kc, col0:col0 + 128], xnT[:, kc, 0:n], kc == 0, kc == 7, WMI_ALL + xnT_names, [PSN(b)])
            evac(b)
        for c in range(4):
            proj(c * 128, lambda b, c=c: act(qT[:, c, :], ps[:, b, :], AF.Copy, [PSN(b)], [QN(c)], scale=0.125))
        for kv in range(2):
            proj(1536 + kv * 128, lambda b, kv=kv: act(kdupT[:, kv, 128:640], ps[:, b, :], AF.Copy, [PSN(b)], KDN))
        for c in range(2):
            proj(768 + c * 128, lambda b, c=c: act(qmT[:, c, :], ps[:, b, :], AF.Copy, [PSN(b)], [QMN(c)], scale=0.125))
        for c in range(2):
            proj(1024 + c * 128, lambda b, c=c: copy("dve", xr[:, c, 3:3 + n], ps[:, b, :], [PSN(b)], [("xr", c)]))
        for c in range(2):
            proj(1280 + c * 128, lambda b, c=c: act(rg_g[:, c, :], ps[:, b, :], AF.Gelu_apprx_tanh, [PSN(b)], [("rg_g", c)]))
        for s in range(4):
            g = ti * 4 + s
            slot = g % 5
            b = bank()
            fin = last and s == 3
            c0, c1 = (512, 768) if fin else (640, 768)
            for kc in range(8):
                mm(ps[:, b, 0:c1 - c0], xnT[:, kc, s * 128:(s + 1) * 128], wmi[:, kc, c0:c1], kc == 0, kc == 7,
                   WMI_ALL + xnT_names, [PSN(b)])
            voff = 128 if fin else 0
            for kv in range(2):
                act(vaug[:, slot, kv, 0, 0:64], ps[:, b, voff + kv * 64: voff + (kv + 1) * 64], AF.Copy, [PSN(b), "vaug_init"], [("vaug", slot)])
                act(vaug[:, slot, kv, 1, 64:128], ps[:, b, voff + kv * 64: voff + (kv + 1) * 64], AF.Copy, [PSN(b), "vaug_init"], [("vaug", slot)])
            if fin:
                act(kvout[:], ps[:, b, 0:256], AF.Copy, [PSN(b)], ["kvout"])
                dma_out(o_swa_k_p, kvout[:, 0:128], ["kvout"], "st_swak")
                dma_out(o_swa_v_p, kvout[:, 128:256], ["kvout"], "st_swav")
        copy("pool", kdupT[:, :, 0:128], kd_hist[:], ["kd_hist"], KDN)
        units = []

        def swa_unit(s, kv):
            g = ti * 4 + s
            st = {}

            def qk():
                st["banks"] = [bank(), bank()]
                pending_banks.update(st["banks"])
                cs = slice(0, 512) if g > 0 else slice(256, 512)
                for p in range(2):
                    mm(ps[:, st["banks"][p], cs], ident[:], mask_po[:, cs], True, False, ["ident", "mask_po"], [PSN(st["banks"][p])], sgc=True)
                for which in (["prev", "own"] if g > 0 else ["own"]):
                    kc0 = s * 128 if which == "prev" else 128 + s * 128
                    c0 = 0 if which == "prev" else 256
                    for p in range(2):
                        b = st["banks"][p]
                        mm(ps[:, b, c0:c0 + 256].rearrange("k (j q) -> k j q", j=2),
                           kdupT[p * 64:(p + 1) * 64, kv, kc0:kc0 + 128],
                           qT[p * 64:(p + 1) * 64, 2 * kv:2 * kv + 2, s * 128:(s + 1) * 128], False, which == "own",
                           KDN + [QN(2 * kv), QN(2 * kv + 1)], [PSN(b)], sgc=True)

            def ex():
                pending_banks.difference_update(st["banks"])
                st["pt"] = []
                cs = slice(0, 512) if g > 0 else slice(256, 512)
                for p in range(2):
                    b = st["banks"][p]
                    pt, ptn = pt_ring.next()
                    act(pt[:, cs], ps[:, b, cs], AF.Exp, [PSN(b)], [ptn])
                    st["pt"].append((pt, ptn))

            def pv():
                b = bank()
                st["bv"] = b
                for p in range(2):
                    cols = slice(p * 256, (p + 1) * 256)
                    pt, ptn = st["pt"][p]
                    if g > 0:
                        mm(ps[:, b, cols], vaug[:, (g - 1) % 5, kv, p, :], pt[:, 0:256], True, False,
                           [("vaug", (g - 1) % 5), ptn, "vaug_init"], [PSN(b)])
                    mm(ps[:, b, cols], vaug[:, g % 5, kv, p, :], pt[:, 256:512], g == 0, False,
                       [("vaug", g % 5), ptn, "vaug_init"], [PSN(b)])
                    mm(ps[:, b, cols], e_sel[:, p, :], sink_full[:, (kv * 2 + p) * 256:(kv * 2 + p + 1) * 256], False, True,
                       ["e_sel", "sink_full"], [PSN(b)])

            def nrm():
                b = st["bv"]
                rd, rdn = rd_ring.next()
                recip(rd[0:64, 0:256], ps[64:128, b, 0:256], [PSN(b)], [rdn])
                recip(rd[64:128, 256:512], ps[0:64, b, 256:512], [PSN(b)], [rdn])
                tt("dve", omixT[0:64, 2 * kv:2 * kv + 2, s * 128:(s + 1) * 128],
                   ps[0:64, b, 0:256].rearrange("p (j q) -> p j q", j=2), rd[0:64, 0:256].rearrange("p (j q) -> p j q", j=2), ALU.mult,
                   [PSN(b), rdn], [OMN(2 * kv), OMN(2 * kv + 1)])
                tt("dve", omixT[64:128, 2 * kv:2 * kv + 2, s * 128:(s + 1) * 128],
                   ps[64:128, b, 256:512].rearrange("p (j q) -> p j q", j=2), rd[64:128, 256:512].rearrange("p (j q) -> p j q", j=2), ALU.mult,
                   [PSN(b), rdn], [OMN(2 * kv), OMN(2 * kv + 1)])
            return qk, ex, pv, nrm

        def mem_unit(mh):
            cm, pm = mh // 2, mh % 2
            prow = slice(pm * 64, (pm + 1) * 64)
            drow = slice((1 - pm) * 64, (2 - pm) * 64)
            st = {}

            def qk():
                st["banks"] = [bank(), bank()]
                pending_banks.update(st["banks"])
                for mc in range(2):
                    b = st["banks"][mc]
                    mm(ps[:, b, 0:n], mkT[prow, cm, mc * 128:(mc + 1) * 128], qmT[prow, cm, 0:n], True, True, ["mkT", QMN(cm)], [PSN(b)])

            def ex():
                pending_banks.difference_update(st["banks"])
                st["pt"] = []
                for mc in range(2):
                    b = st["banks"][mc]
                    pt, ptn = pt_ring.next()
                    act(pt, ps[:, b, :], AF.Exp, [PSN(b)], [ptn])
                    st["pt"].append((pt, ptn))

            def pv():
                b = bank()
                st["bv"] = b
                for mc in range(2):
                    pt, ptn = st["pt"][mc]
                    mm(ps[:, b, 0:n], mvaug[:, mc, mh, :], pt, mc == 0, mc == 1, ["mvaug", ptn], [PSN(b)])

            def nrm():
                b = st["bv"]
                rd, rdn = rd_ring.next()
                recip(rd[prow, :], ps[drow, b, :], [PSN(b)], [rdn])
                tt("dve", omixT[prow, 4 + cm, :], ps[prow, b, :], rd[prow, :], ALU.mult, [PSN(b), rdn], [OMN(4 + cm)])
            return qk, ex, pv, nrm

        for s in range(4):
            for kv in range(2):
                units.append(swa_unit(s, kv))
        for mh in range(4):
            units.append(mem_unit(mh))

        def rg_a_part(c):
            ts("dve", rg_u, xr[:, c, 0:n], convw[:, c, 0:1], rgc[:, 0, c:c + 1], ALU.mult, ALU.add,
               [("xr", c), "xr_hist", ("convw", c), ("rgc", 0)], ["rg_u"])
            for j in range(1, 4):
                stt(rg_u, xr[:, c, j:j + n], convw[:, c, j:j + 1], rg_u, ALU.mult, ALU.add,
                    [("xr", c), "xr_hist", ("convw", c), "rg_u"], ["rg_u"])
            act(rg_ub[:], rg_u, AF.Copy, ["rg_u"], ["rg_ub"])

        def rg_b_part(c):
            ba = bank()
            mm(ps[:, ba, :], wbd[:, 0, c, :], rg_ub[:], True, True, ["wbd", "rg_ub"], [PSN(ba)])
            bx = bank()
            mm(ps[:, bx, :], wbd[:, 1, c, :], rg_ub[:], True, True, ["wbd", "rg_ub"], [PSN(bx)])
            act(rg_r, ps[:, ba, :], AF.Sigmoid, [PSN(ba), ("rgc", 1)], ["rg_r"], bias=rgc[:, 1, c:c + 1])
            act(rg_i[:], ps[:, bx, :], AF.Sigmoid, [PSN(bx), ("rgc", 2)], ["rg_i"], bias=rgc[:, 2, c:c + 1])
            act(rg_a[:], rg_r, AF.Exp, ["rg_r", ("rgc", 4)], ["rg_a"], scale=rgc[:, 4, c:c + 1])
            act(rg_r, rg_r, AF.Exp, ["rg_r", ("rgc", 5)], ["rg_r"], scale=rgc[:, 5, c:c + 1])
            act(rg_r, rg_r, AF.Sqrt, ["rg_r"], ["rg_r"], scale=-1.0, bias=1.0)

        def rg_c_part(c):
            tt("dve", rg_i[:], rg_i[:], rg_u, ALU.mult, ["rg_i", "rg_u"], ["rg_i"])
            tt("dve", rg_i[:], rg_i[:], rg_r, ALU.mult, ["rg_i", "rg_r"], ["rg_i"])
            P.add("dve", (lambda c: lambda h: h.tensor_tensor_scan(out=hbuf[:, c, 1:1 + n], data0=rg_a[:], data1=rg_i[:],
                                                                  initial=hbuf[:, c, 0:1], op0=ALU.mult, op1=ALU.add))(c),
                  reads=["rg_a", "rg_i", "h_carry"], writes=[("h", c)])
            tt("dve", omixT[:, 6 + c, :], hbuf[:, c, 1:1 + n], rg_g[:, c, :], ALU.mult, [("h", c), ("rg_g", c)], [OMN(6 + c)])
        extra = {0: lambda: rg_a_part(0), 1: lambda: rg_b_part(0), 2: lambda: rg_c_part(0),
                 3: lambda: rg_a_part(1), 4: lambda: rg_b_part(1), 5: lambda: rg_c_part(1)}

        units[0][0]()
        units[1][0]()
        units[0][1]()
        for i, (qk, ex, pv, nrm) in enumerate(units):
            if i + 2 < len(units):
                units[i + 2][0]()
            if i + 1 < len(units):
                units[i + 1][1]()
            pv()
            nrm()
            if i in extra:
                extra[i]()
            if i == 5:
                copy("pool", kd_hist[:], kdupT[:, :, 512:640], KDN, ["kd_hist"])
        if last:
            for c in range(2):
                dma_out(o_conv_p[:, c * 128:(c + 1) * 128].rearrange("j p -> p j"), xr[:, c, n:n + 3], [("xr", c)], "st_convp%d" % c, nonc=True)
                dma_out(o_h_p[c * 128:(c + 1) * 128].rearrange("(p o) -> p o", o=1), hbuf[:, c, n:n + 1], [("h", c)], "st_hp%d" % c, nonc=True)
        else:
            copy("pool", xr[:, :, 0:3], xr[:, :, n:n + 3], [("xr", 0), ("xr", 1)], ["xr_hist"])
            copy("pool", hbuf[:, :, 0:1], hbuf[:, :, n:n + 1], [("h", 0), ("h", 1)], ["h_carry"])
        om_names = [OMN(i) for i in range(8)]

        def group(s):
            pk = pair()
            for half in range(2):
                for ch in range(8):
                    mm(ps[:, pk + half, :], omixT[:, ch, s * 128:(s + 1) * 128], wmo[:, ch, half * 512:(half + 1) * 512],
                       ch == 0, ch == 7, om_names + ALLBIG + [("wmo", 0), ("wmo", 1)], [PSN(pk + half)])
            return pk
        out_loop(4, 128, group, 1, False, after_sub)

    def mixer_sample(reload_weights=True):
        run_deferred()
        n = DB
        xnT = xnT_s
        prenorm_sub(0, x_s[:], "x_s", n, 1, xnT, "xnT_s")
        xnT_names = ["xnT_s"]
        if reload_weights:
            load_mix_weights()
        old_names = [("rg_g", 0), ("rg_g", 1), "rg_a", "rg_i", "rg_u", "rg_r", "rg_ub", "kd_hist", "mkT", "mvaug", "vaug_init", "mask_po", "junk",
                     "xr_hist", "h_carry", ("xr", 0), ("xr", 1), ("h", 0), ("h", 1), "kvout"] + [("vaug", i) for i in range(5)]
        ksf_ring = Ring("s_ksf", [rg_g[:, 0, 0:128], rg_g[:, 0, 128:256]])
        vsf_ring = Ring("s_vsf", [rg_g[:, 0, 256:384], rg_g[:, 0, 384:512]])
        mksf_ring = Ring("s_mksf", [rg_a[:].rearrange("p (m f) -> p m f", m=2), rg_u.rearrange("p (m f) -> p m f", m=2)])
        mvsf_ring = Ring("s_mvsf", [rg_i[:].rearrange("p (m f) -> p m f", m=2), rg_r.rearrange("p (m f) -> p m f", m=2)])
        kdup_ring = Ring("s_kdup", [rg_ub[:, 0:256].rearrange("p (k c) -> p k c", k=2), rg_ub[:, 256:512].rearrange("p (k c) -> p k c", k=2)])
        vx_ring = Ring("s_vx", [hbuf[0:1, 0, 0:128], hbuf[0:1, 0, 128:256]])
        vxa_ring = Ring("s_vxa", [vxaug[0:1, i, :].rearrange("o (k f c) -> o k f c", k=2, f=2) for i in range(2)])
        s_xr = xr[:, :, 0:n]
        s_cv = xr[:, :, 16:64].rearrange("p c (j b) -> p c j b", j=3)
        s_h0 = xr[:, :, 64:80]
        s_u = xr[:, :, 80:96]
        s_r = xr[:, :, 96:112]
        s_i = xr[:, :, 112:128]
        s_a = xr[:, :, 128:144]
        s_h = xr[:, :, 144:160]
        s_g = xr[:, :, 160:176]
        new_names = ([("s_ksf", i) for i in range(2)] + [("s_vsf", i) for i in range(2)] + [("s_mksf", i) for i in range(2)] +
                     [("s_mvsf", i) for i in range(2)] + [("s_kdup", i) for i in range(2)] + [("s_vx", i) for i in range(2)] +
                     [("s_vxa", i) for i in range(2)] + ["s_xr", "s_cv", "s_h0", "s_u", "s_r", "s_i", "s_a", "s_h", "s_g", "s_kvtok", "s_mkT", "s_mvaug"] +
                     [("s_vaug", i) for i in range(5)] +
                     [(nm, c) for nm in ("s_u", "s_r", "s_i", "s_a", "s_h", "s_ub") for c in range(2)])
        P.add("pool", lambda h: h.memset(vxaug[:], 1.0), writes=old_names + new_names)
        mkT_slots = [(mkT, "s_mkT"), (mask_po[:].rearrange("p (c m) -> p c m", c=2), "mask_po")]
        mvaug_slots = [(mvaug[:], "s_mvaug"), (junk[:].rearrange("p (m h c) -> p m h c", m=2, h=4), "junk")]
        P.add("pool", lambda h: h.memset(junk[:], 1.0), reads=["junk"], writes=["junk"])

        def proj(col0, evac):
            b = bank()
            for kc in range(8):
                mm(ps[:, b, 0:n], wmi[:, kc, col0:col0 + 128], xnT[:, kc, 0:n], kc == 0, kc == 7, WMI_ALL + xnT_names, [PSN(b)])
            evac(b)
        for c in range(4):
            proj(c * 128, lambda b, c=c: act(qT[:, c, 0:n], ps[:, b, 0:n], AF.Copy, [PSN(b)], [QN(c)], scale=0.125))
        for kv in range(2):
            proj(1536 + kv * 128, lambda b, kv=kv: act(kdupT[:, kv, 128:128 + n], ps[:, b, 0:n], AF.Copy, [PSN(b)], KDN))
        for c in range(2):
            proj(768 + c * 128, lambda b, c=c: act(qmT[:, c, 0:n], ps[:, b, 0:n], AF.Copy, [PSN(b)], [QMN(c)], scale=0.125))
        for c in range(2):
            proj(1024 + c * 128, lambda b, c=c: copy("dve", s_xr[:, c, :], ps[:, b, 0:n], [PSN(b)], ["s_xr"]))
        for c in range(2):
            proj(1280 + c * 128, lambda b, c=c: act(s_g[:, c, :], ps[:, b, 0:n], AF.Gelu_apprx_tanh, [PSN(b)], ["s_g"]))
        b = bank()
        for kc in range(8):
            mm(ps[0:n, b, 0:256], xnT[:, kc, 0:n], wmi[:, kc, 512:768], kc == 0, kc == 7, WMI_ALL + xnT_names, [PSN(b)])
        act(kvout[0:n, :], ps[0:n, b, 0:256], AF.Copy, [PSN(b)], ["s_kvtok"])
        P.add("sp", lambda h: h.dma_start(out=o_swa_k_s[:, 0:127, :], in_=cache_swa_k[:, 1:128, :]), dma_key="st_sk_shift", final=True)
        P.add("sp", lambda h: h.dma_start(out=o_swa_v_s[:, 0:127, :], in_=cache_swa_v[:, 1:128, :]), dma_key="st_sv_shift", final=True)
        dma_out(o_swa_k_s[:, 127, :], kvout[0:n, 0:128], ["s_kvtok"], "st_sk_new")
        dma_out(o_swa_v_s[:, 127, :], kvout[0:n, 128:256], ["s_kvtok"], "st_sv_new")
        P.add("sp", lambda h: h.dma_start(out=o_conv_s[:, 0:2, :], in_=state_conv[:, 1:3, :]), dma_key="st_cv_shift", final=True)
        for bi in range(n):
            ksf, ksn = ksf_ring.next()
            vsf, vsn = vsf_ring.next()
            mksf, mksn = mksf_ring.next()
            mvsf, mvsn = mvsf_ring.next()
            dma_in(ksf, cache_swa_k[bi], [ksn], ksn)
            dma_in(vsf, cache_swa_v[bi], [vsn], vsn)
            dma_in(mksf, cache_mem_k[bi].rearrange("(m p) f -> p m f", p=128), [mksn], mksn)
            dma_in(mvsf, cache_mem_v[bi].rearrange("(m p) f -> p m f", p=128), [mvsn], mvsn)
            vx, vxn = vx_ring.next()
            P.add("sp", (lambda vx, bi: lambda h: h.dma_start(out=vx, in_=kvout[bi:bi + 1, 128:256]))(vx, bi), reads=["s_kvtok"], writes=[vxn], dma_key=vxn)
            kdup, kdn = kdup_ring.next()
            slot = bi % 5
            vaug_n = ("s_vaug", slot)
            ksv = ksf.rearrange("p (k d) -> p k d", k=2)
            vsv = vsf.rearrange("p (k d) -> p k d", k=2)
            copy("pool", kdup[:, :, 0:64], ksv, [ksn], [kdn])
            copy("pool", kdup[:, :, 64:128], ksv, [ksn, kdn], [kdn])
            copy("pool", vaug[:, slot, :, 0, 0:64], vsv, [vsn], [vaug_n])
            copy("pool", vaug[:, slot, :, 1, 64:128], vsv, [vsn, vaug_n], [vaug_n])
            vxa, vxan = vxa_ring.next()
            vxv = vx.rearrange("o (k d) -> o k d", k=2)
            copy("pool", vxa[:, :, 0, 0:64], vxv, [vxn], [vxan])
            copy("pool", vxa[:, :, 1, 64:128], vxv, [vxn, vxan], [vxan])
            mkb, mkbn = xn_ring.next()
            mkbv = mkb[:, 0:512].rearrange("p (m f) -> p m f", m=2)
            copy("pool", mkbv, mksf, [mksn], [mkbn])
            mkT_b, mkT_n = mkT_slots[bi % 2]
            mvaug_b, mvaug_n = mvaug_slots[bi % 2]
            mva = mvaug_b.rearrange("p m (a two) c -> p m a two c", two=2)
            mvs = mvsf.rearrange("p m (a two d) -> p m a two d", two=2, d=64)
            for pm in range(2):
                cols = slice(0, 64) if pm == 0 else slice(64, 128)
                for mc in range(2):
                    copy("pool", mva[:, mc, :, pm, cols], mvs[:, mc, :, pm, :], [mvsn, mvaug_n], [mvaug_n])
            k = psb_next()
            for kv in range(2):
                P.add("pe", (lambda k, kv, kdup: lambda h: h.transpose(out=psb[:, k, kv * 128:(kv + 1) * 128], in_=kdup[:, kv, :], identity=ident[:]))(k, kv, kdup),
                      reads=[kdn, "ident"], writes=[("psb", k)])
            for mc in range(2):
                for cm in range(2):
                    P.add("pe", (lambda k, mc, cm, mkbv: lambda h: h.transpose(out=psb[:, k, 256 + cm * 256 + mc * 128: 256 + cm * 256 + (mc + 1) * 128],
                                                                             in_=mkbv[:, mc, cm * 128:(cm + 1) * 128], identity=ident[:]))(k, mc, cm, mkbv),
                          reads=[mkbn, "ident"], writes=[("psb", k)])
            ktt, kttn = pt_ring.next()
            kT = ktt[:, 0:256].rearrange("p (k s) -> p k s", k=2)
            copy("dve", ktt[:, 0:256], psb[:, k, 0:256], [("psb", k)], [kttn])
            copy("dve", mkT_b[:, :, :].rearrange("p c m -> p (c m)") if bi % 2 == 0 else mask_po[:], psb[:, k, 256:768], [("psb", k)], [mkT_n])
            pk = pair()
            for kv in range(2):
                for p in range(2):
                    pr = slice(p * 64, (p + 1) * 64)
                    mm(ps[:, pk + p, kv * 2:kv * 2 + 2], kT[pr, kv, :], qT[pr, 2 * kv:2 * kv + 2, bi], True, True,
                       [kttn, QN(2 * kv), QN(2 * kv + 1)], [PSN(pk + p)])
                    mm(ps[0:1, pk + p, 8 + kv * 2:8 + kv * 2 + 2], kdupT[pr, kv, 128 + bi:128 + bi + 1], qT[pr, 2 * kv:2 * kv + 2, bi], True, True,
                       KDN + [QN(2 * kv), QN(2 * kv + 1)], [PSN(pk + p)])
            for mh in range(4):
                cm, pm = mh // 2, mh % 2
                pr = slice(pm * 64, (pm + 1) * 64)
                for mc in range(2):
                    c0 = 16 + cm * 2 + mc
                    mm(ps[:, pk + pm, c0:c0 + 1], mkT_b[pr, cm, mc * 128:(mc + 1) * 128], qmT[pr, cm, bi:bi + 1], True, True,
                       [mkT_n, QMN(cm)], [PSN(pk + pm)])
            pmi, pmin = pt_ring.next()
            pT = pmi[:, 0:8].rearrange("p (a c) -> p a c", a=2)
            pX = pmi[0:1, 8:16].rearrange("p (a c) -> p a c", a=2)
            pM = pmi[:, 16:24].rearrange("p (a c) -> p a c", a=2)
            act(pT, ps[:, pk:pk + 2, 0:4], AF.Exp, [PSN(pk), PSN(pk + 1)], [pmin])
            act(pX, ps[0:1, pk:pk + 2, 8:12], AF.Exp, [PSN(pk), PSN(pk + 1), pmin], [pmin])
            act(pM, ps[:, pk:pk + 2, 16:20], AF.Exp, [PSN(pk), PSN(pk + 1), pmin], [pmin])
            bv = bank()
            for kv in range(2):
                for p in range(2):
                    c0 = p * 4 + kv * 2
                    mm(ps[:, bv, c0:c0 + 2], vaug[:, slot, kv, p, :], pT[:, p, kv * 2:kv * 2 + 2], True, False, [vaug_n, pmin], [PSN(bv)])
                    mm(ps[:, bv, c0:c0 + 2], vxa[0:1, kv, p, :], pX[0:1, p, kv * 2:kv * 2 + 2], False, False, [vxan, pmin], [PSN(bv)])
                    base = (kv * 2 + p) * 256
                    mm(ps[:, bv, c0:c0 + 2], e_sel[:, p, :], sink_full[:, base:base + 256].rearrange("p (j q) -> p j q", j=2)[:, :, 0], False, True,
                       ["e_sel", "sink_full"], [PSN(bv)])
            for mh in range(4):
                cm, pm = mh // 2, mh % 2
                for mc in range(2):
                    mm(ps[:, bv, 8 + mh:9 + mh], mvaug_b[:, mc, mh, :], pM[:, pm, cm * 2 + mc:cm * 2 + mc + 1], mc == 0, mc == 1, [mvaug_n, pmin], [PSN(bv)])
            rd, rdn = rd_ring.next()
            for p in range(2):
                orow = slice(p * 64, (p + 1) * 64)
                drow = slice((1 - p) * 64, (2 - p) * 64)
                recip(rd[orow, p * 4:p * 4 + 4], ps[drow, bv, p * 4:p * 4 + 4], [PSN(bv)], [rdn])
                tt("dve", omixT[orow, 0:4, bi], ps[orow, bv, p * 4:p * 4 + 4], rd[orow, p * 4:p * 4 + 4], ALU.mult, [PSN(bv), rdn],
                   [OMN(0), OMN(1), OMN(2), OMN(3)])
            for mh in range(4):
                cm, pm = mh // 2, mh % 2
                orow = slice(pm * 64, (pm + 1) * 64)
                drow = slice((1 - pm) * 64, (2 - pm) * 64)
                recip(rd[orow, 8 + mh:9 + mh], ps[drow, bv, 8 + mh:9 + mh], [PSN(bv)], [rdn])
                tt("dve", omixT[orow, 4 + cm, bi:bi + 1], ps[orow, bv, 8 + mh:9 + mh], rd[orow, 8 + mh:9 + mh], ALU.mult, [PSN(bv), rdn], [OMN(4 + cm)])
        for c in range(2):
            for j in range(3):
                dma_in_nc(s_cv[:, c, j, :], state_conv[:, j, c * 128:(c + 1) * 128].rearrange("b p -> p b"), ["s_cv"], "s_cvld%d%d" % (c, j))
            dma_in_nc(s_h0[:, c, :], state_h[:, c * 128:(c + 1) * 128].rearrange("b p -> p b"), ["s_h0"], "s_h0ld%d" % c)
        s_ubf = kd_hist[:, :, 0:n]
        for c in range(2):
            ts("dve", s_u[:, c, :], s_cv[:, c, 0, :], convw[:, c, 0:1], rgc[:, 0, c:c + 1], ALU.mult, ALU.add, ["s_cv", ("convw", c), ("rgc", 0)], [("s_u", c)])
            for j in range(1, 3):
                stt(s_u[:, c, :], s_cv[:, c, j, :], convw[:, c, j:j + 1], s_u[:, c, :], ALU.mult, ALU.add, ["s_cv", ("convw", c), ("s_u", c)], [("s_u", c)])
            stt(s_u[:, c, :], s_xr[:, c, :], convw[:, c, 3:4], s_u[:, c, :], ALU.mult, ALU.add, ["s_xr", ("convw", c), ("s_u", c)], [("s_u", c)])
            act(s_ubf[:, c, :], s_u[:, c, :], AF.Copy, [("s_u", c)], [("s_ub", c)])
            ba = bank()
            mm(ps[:, ba, 0:n], wbd[:, 0, c, :], s_ubf[:, c, :], True, True, ["wbd", ("s_ub", c)], [PSN(ba)])
            bx = bank()
            mm(ps[:, bx, 0:n], wbd[:, 1, c, :], s_ubf[:, c, :], True, True, ["wbd", ("s_ub", c)], [PSN(bx)])
            act(s_r[:, c, :], ps[:, ba, 0:n], AF.Sigmoid, [PSN(ba), ("rgc", 1)], [("s_r", c)], bias=rgc[:, 1, c:c + 1])
            act(s_i[:, c, :], ps[:, bx, 0:n], AF.Sigmoid, [PSN(bx), ("rgc", 2)], [("s_i", c)], bias=rgc[:, 2, c:c + 1])
            act(s_a[:, c, :], s_r[:, c, :], AF.Exp, [("s_r", c), ("rgc", 4)], [("s_a", c)], scale=rgc[:, 4, c:c + 1])
            act(s_r[:, c, :], s_r[:, c, :], AF.Exp, [("s_r", c), ("rgc", 5)], [("s_r", c)], scale=rgc[:, 5, c:c + 1])
            act(s_r[:, c, :], s_r[:, c, :], AF.Sqrt, [("s_r", c)], [("s_r", c)], scale=-1.0, bias=1.0)
            tt("dve", s_i[:, c, :], s_i[:, c, :], s_u[:, c, :], ALU.mult, [("s_i", c), ("s_u", c)], [("s_i", c)])
            tt("dve", s_i[:, c, :], s_i[:, c, :], s_r[:, c, :], ALU.mult, [("s_i", c), ("s_r", c)], [("s_i", c)])
            tt("dve", s_h[:, c, :], s_a[:, c, :], s_h0[:, c, :], ALU.mult, [("s_a", c), "s_h0"], [("s_h", c)])
            tt("dve", s_h[:, c, :], s_h[:, c, :], s_i[:, c, :], ALU.add, [("s_h", c), ("s_i", c)], [("s_h", c)])
            tt("dve", omixT[:, 6 + c, 0:n], s_h[:, c, :], s_g[:, c, :], ALU.mult, [("s_h", c), "s_g"], [OMN(6 + c)])
            dma_out(o_h_s[:, c * 128:(c + 1) * 128].rearrange("b p -> p b"), s_h[:, c, :], [("s_h", c)], "st_hs%d" % c, nonc=True)
            dma_out(o_conv_s[:, 2, c * 128:(c + 1) * 128].rearrange("b p -> p b"), s_xr[:, c, :], ["s_xr"], "st_cvs%d" % c, nonc=True)
        om_names = [OMN(i) for i in range(8)]
        pk = pair()
        for half in range(2):
            for ch in range(8):
                mm(ps[0:n, pk + half, :], omixT[:, ch, 0:n], wmo[:, ch, half * 512:(half + 1) * 512],
                   ch == 0, ch == 7, om_names + ALLBIG + [("wmo", 0), ("wmo", 1)], [PSN(pk + half)])
        postnorm_residual(pk, 0, n, 1, False, xbuf=x_s[:], xname="x_s")

    if stop_after == "const":
        P.run()
        return nc
    if do_sample:
        P.add("act", lambda h: h.dma_start(out=x_s[:], in_=x_sample), writes=["x_s", "cst_chain"], dma_key="cst")
    mem_kv_prompt()
    if stop_after == "memkv":
        P.run()
        return nc
    def switch_to_cached():
        flush_stores(0)
        xs = []
        for i in range(2):
            sv = stage[i][:].bitcast(BF16)
            for hlf in range(2):
                xs.append(sv[:, hlf * 4096:(hlf + 1) * 4096].rearrange("p (k f) -> p k f", k=8))
        items = list(wslab) + xs

        class WR:
            def __init__(self):
                self.i = 0

            def next(self):
                k = self.i % 6
                self.i += 1
                return items[k], (("wslab", k) if k < 2 else ("wslabx", k - 2))
        wsr[0] = WR()
        cached_mode[0] = True

    pre_state = {}

    def pre_sub(gi):
        return lambda s: prenorm_sub(s, x_res[:, s, :], ("x", s), 128, gi, xnT, ("xnT", s))

    def pre_hooks(gi):
        def ha(s):
            pre_state[s] = prenorm_a(x_res[:, s, :], ("x", s), 128)

        def hb(s):
            prenorm_sub(s, x_res[:, s, :], ("x", s), 128, gi, xnT, ("xnT", s), pre=pre_state.pop(s))
        return ha, hb

    for ti in range(n_ptiles):
        t0 = ti * NT
        last = ti == n_ptiles - 1
        npf = 2 if ti == 0 else 5
        if ti == 0:
            for s in range(4):
                dma_in(x_res[:, s, :], x_prompt[t0 + s * 128:t0 + (s + 1) * 128, :], [("x", s)], ("xld", s))
            for s in range(4):
                pre_sub(0)(s)
        ride = last and do_sample
        ffn(0, 0, 0, NT, 4, 128, do_pre=False, after_sub=pre_hooks(1), ride=ride)
        mixer_prompt(ti, last, do_pre=False, after_sub=pre_hooks(2), at_start=lambda: prefetch_slabs(1, npf))
        if ride:
            mixer_sample(reload_weights=False)

        def after2a(s, t0=t0, last=last):
            dma_out(y_prompt[t0 + s * 128:t0 + (s + 1) * 128, :], x_res[:, s, :], [("x", s)], ("yst", s))
            if not last:
                dma_in(x_res[:, s, :], x_prompt[t0 + NT + s * 128:t0 + NT + (s + 1) * 128, :], [("x", s)], ("xld", s))
                pre_state[s] = prenorm_a(x_res[:, s, :], ("x", s), 128)

        def after2b(s, last=last):
            if not last:
                prenorm_sub(s, x_res[:, s, :], ("x", s), 128, 0, xnT, ("xnT", s), pre=pre_state.pop(s))
        after2 = (after2a, after2b)
        def before2(ti=ti):
            if ti == 0:
                switch_to_cached()
            prefetch_slabs(0, 5)
        ffn(1, 2, 2, NT, 4, 128, do_pre=False, after_sub=after2, before_out=before2 if not last else None, ride=ride)
        if ride:
            dma_out(y_sample, x_s[:], ["x_s"], "yst_s")
    run_deferred()
    P.run()
    return nc


_NC_CACHE = {}


def _in_maps(inputs):
    f = lambda a: np.ascontiguousarray(np.asarray(a, dtype=np.float32))
    shared = {}
    for k in ["ln_ffn1_pre", "ln_ffn1_post", "w_ffn1_in", "w_ffn1_out", "ln_mix_pre", "ln_mix_post", "w_in", "w_out",
              "swa_sinks", "conv_w", "conv_b", "rg_wa", "rg_ba", "rg_wx", "rg_bx", "rg_lambda", "ln_mem", "w_mem_kv",
              "ln_ffn2_pre", "ln_ffn2_post", "w_ffn2_in", "w_ffn2_out"]:
        a = f(inputs[k])[0]
        if k in ("rg_ba", "rg_bx"):
            a = a.reshape(256)
        shared[k] = np.ascontiguousarray(a)
    maps = []
    for c in range(NCORES):
        sl = slice(c * DB, (c + 1) * DB)
        m = dict(shared)
        m["x_prompt"] = f(inputs["x_prompt"][c])
        m["x_sample"] = f(inputs["x_sample"][sl, 0])
        m["mem_prompt"] = f(inputs["mem_prompt"][c])
        m["cache_swa_k"] = f(inputs["cache_swa_k"][0, sl]).reshape(DB, 128, 128)
        m["cache_swa_v"] = f(inputs["cache_swa_v"][0, sl]).reshape(DB, 128, 128)
        m["cache_mem_k"] = f(inputs["cache_mem_k"][0, sl]).reshape(DB, 256, 256)
        m["cache_mem_v"] = f(inputs["cache_mem_v"][0, sl]).reshape(DB, 256, 256)
        m["state_conv"] = f(inputs["state_conv"][0, sl])
        m["state_rglru_h"] = f(inputs["state_rglru_h"][0, sl])
        maps.append(m)
    return maps


def _gather(res):
    R = res.results
    cat = lambda k: np.stack([np.asarray(R[c][k]) for c in range(NCORES)])
    y_prompt = cat("y_prompt")
    y_sample = cat("y_sample").reshape(128, 1, D)
    swa_k_p = cat("swa_k_p").reshape(1, 8, 128, 2, 64)
    swa_v_p = cat("swa_v_p").reshape(1, 8, 128, 2, 64)
    mem_k_p = cat("mem_k_p").reshape(1, 8, 256, 4, 64)
    mem_v_p = cat("mem_v_p").reshape(1, 8, 256, 4, 64)
    conv_p = cat("conv_p").reshape(1, 8, 3, 256)
    h_p = cat("h_p").reshape(1, 8, 256)
    swa_k_s = cat("swa_k_s").reshape(1, 128, 128, 2, 64)
    swa_v_s = cat("swa_v_s").reshape(1, 128, 128, 2, 64)
    conv_s = cat("conv_s").reshape(1, 128, 3, 256)
    h_s = cat("h_s").reshape(1, 128, 256)
    return tuple(np.ascontiguousarray(a, dtype=np.float32) for a in
                 (y_prompt, y_sample, swa_k_p, swa_v_p, mem_k_p, mem_v_p, conv_p, h_p, swa_k_s, swa_v_s, conv_s, h_s))


def kernel(**inputs):
    if "nc" not in _NC_CACHE:
        _NC_CACHE["nc"] = build()
    nc = _NC_CACHE["nc"]
    res = run_bass_kernel_spmd(nc, _in_maps(inputs), core_ids=list(range(NCORES)))
    return _gather(res)
```

```python
import numpy as np
import concourse.bass as bass
import concourse.mybir as mybir
from concourse.bass_utils import run_bass_kernel_spmd

F32 = mybir.dt.float32
BF16 = mybir.dt.bfloat16
AF = mybir.ActivationFunctionType
ALU = mybir.AluOpType

D = 1024
T = 4096
DFF = 2816
NFC = 22
NT = 512
EPS = 1e-6
NCORES = 8
DB = 16
LAG = 2


class Ins:
    __slots__ = ("eng", "fn", "idx", "deps", "is_dma", "dma_key", "tok_val",
                 "needs_inc", "waits")

    def __init__(self, eng, fn, idx, is_dma, dma_key):
        self.eng = eng
        self.fn = fn
        self.idx = idx
        self.is_dma = is_dma
        self.dma_key = dma_key
        self.deps = []
        self.tok_val = None
        self.needs_inc = False
        self.waits = None


class Prog:
    ENGS = ("pe", "act", "dve", "pool", "sp")

    def __init__(self, nc):
        self.nc = nc
        self.lists = {e: [] for e in self.ENGS}
        self.last_w = {}
        self.readers = {}
        self.dma_count = {}
        self.final_dma = []
        self.expand = {}

    def add(self, eng, fn, reads=(), writes=(), dma_key=None, final=False):
        is_dma = dma_key is not None
        ins = Ins(eng, fn, len(self.lists[eng]), is_dma, dma_key)
        deps = {}
        if self.expand:
            reads = [x for r in reads for x in self.expand.get(r, (r,))]
            writes = [x for w in writes for x in self.expand.get(w, (w,))]
        psum_reads = [r for r in reads if isinstance(r, tuple) and r[0] in ("ps", "psb")] if eng != "pe" else []

        def dep(d, kind):
            if d is ins:
                return
            k = id(d)
            if k in deps:
                if kind != "war":
                    deps[k] = (d, kind)
            else:
                deps[k] = (d, kind)

        for r in reads:
            lw = self.last_w.get(r)
            if lw is not None:
                dep(lw, "raw")
        for r in psum_reads:
            rd = self.readers.get(r)
            if rd:
                for d in rd["eng"].values():
                    if d.eng != eng:
                        dep(d, "raw")
        for w in writes:
            lw = self.last_w.get(w)
            if lw is not None:
                dep(lw, "waw")
            rd = self.readers.get(w)
            if rd:
                for d in rd["eng"].values():
                    dep(d, "war")
                for d in rd["dma"]:
                    dep(d, "war")
        for r in reads:
            rd = self.readers.setdefault(r, {"eng": {}, "dma": []})
            if is_dma:
                rd["dma"].append(ins)
            else:
                rd["eng"][eng] = ins
        for w in writes:
            self.last_w[w] = ins
            self.readers[w] = {"eng": {}, "dma": []}
        ins.deps = list(deps.values())
        if is_dma:
            c = self.dma_count.get(dma_key, 0) + 1
            self.dma_count[dma_key] = c
            ins.tok_val = 16 * c
            if final:
                self.final_dma.append(ins)
        self.lists[eng].append(ins)
        return ins

    def finalize(self):
        for e in self.ENGS:
            waited_idx = {}
            waited_dma = {}
            for ins in self.lists[e]:
                w_eng = {}
                w_dma = {}
                for d, kind in ins.deps:
                    if d.is_dma:
                        if waited_dma.get(d.dma_key, 0) >= d.tok_val:
                            continue
                        if w_dma.get(d.dma_key, 0) < d.tok_val:
                            w_dma[d.dma_key] = d.tok_val
                    else:
                        if d.eng == e and not ins.is_dma:
                            if e == "pe":
                                continue
                        if waited_idx.get(d.eng, -1) >= d.idx:
                            continue
                        cur = w_eng.get(d.eng)
                        if cur is None or cur.idx < d.idx:
                            w_eng[d.eng] = d
                for k, v in w_dma.items():
                    waited_dma[k] = v
                for se, d in w_eng.items():
                    waited_idx[se] = d.idx
                    d.needs_inc = True
                ins.waits = (w_eng, w_dma)
        for e in self.ENGS:
            c = 0
            for ins in self.lists[e]:
                if ins.is_dma:
                    continue
                if ins.needs_inc:
                    c += 1
                    ins.tok_val = c

    def emit_engine(self, e, handle, sems, dma_sems):
        for ins in self.lists[e]:
            w_eng, w_dma = ins.waits
            for se, d in w_eng.items():
                handle.wait_ge(sems[se], d.tok_val)
            for k, v in w_dma.items():
                handle.wait_ge(dma_sems[k], v)
            r = ins.fn(handle)
            if ins.is_dma:
                r.then_inc(dma_sems[ins.dma_key], 16)
            elif ins.needs_inc:
                r.then_inc(sems[e], 1)
        if e == "sp":
            done = set()
            for ins in self.final_dma:
                if ins.dma_key in done:
                    continue
                done.add(ins.dma_key)
                handle.wait_ge(dma_sems[ins.dma_key], self.dma_count[ins.dma_key] * 16)

    def run(self):
        nc = self.nc
        sems = {e: nc.alloc_semaphore("sem_" + e) for e in self.ENGS}
        dma_sems = {k: nc.alloc_semaphore("dsem_%d" % i) for i, k in enumerate(self.dma_count)}
        self.finalize()
        with nc.Block() as block:
            @block.tensor
            def _(h):
                self.emit_engine("pe", h, sems, dma_sems)

            @block.scalar
            def _(h):
                self.emit_engine("act", h, sems, dma_sems)

            @block.vector
            def _(h):
                self.emit_engine("dve", h, sems, dma_sems)

            @block.gpsimd
            def _(h):
                self.emit_engine("pool", h, sems, dma_sems)

            @block.sync
            def _(h):
                self.emit_engine("sp", h, sems, dma_sems)


class Ring:
    def __init__(self, name, items, off=0):
        self.name = name
        self.items = items
        self.i = 0
        self.off = off

    def next(self):
        k = self.i % len(self.items)
        self.i += 1
        return self.items[k], (self.name, k + self.off)


def build(n_ptiles=8, do_sample=True, stop_after=None):
    nc = bass.Bass("TRN2", target_bir_lowering=False)

    def din(name, shape):
        return nc.dram_tensor(name, shape, F32, kind="ExternalInput").ap()

    def dout(name, shape):
        return nc.dram_tensor(name, shape, F32, kind="ExternalOutput").ap()

    x_prompt = din("x_prompt", [T, D])
    x_sample = din("x_sample", [DB, D])
    mem_prompt = din("mem_prompt", [256, D])
    cache_swa_k = din("cache_swa_k", [DB, 128, 128])
    cache_swa_v = din("cache_swa_v", [DB, 128, 128])
    cache_mem_k = din("cache_mem_k", [DB, 256, 256])
    cache_mem_v = din("cache_mem_v", [DB, 256, 256])
    state_conv = din("state_conv", [DB, 3, 256])
    state_h = din("state_rglru_h", [DB, 256])
    ln_pre = [din("ln_ffn1_pre", [D]), din("ln_mix_pre", [D]), din("ln_ffn2_pre", [D]), din("ln_mem", [D])]
    ln_post = [din("ln_ffn1_post", [D]), din("ln_mix_post", [D]), din("ln_ffn2_post", [D])]
    w_ffn_in = [din("w_ffn1_in", [D, 2 * DFF]), din("w_ffn2_in", [D, 2 * DFF])]
    w_ffn_out = [din("w_ffn1_out", [DFF, D]), din("w_ffn2_out", [DFF, D])]
    w_in = din("w_in", [D, 1536])
    w_out = din("w_out", [D, D])
    swa_sinks = din("swa_sinks", [8])
    conv_w = din("conv_w", [4, 256])
    conv_b = din("conv_b", [256])
    rg_wa = din("rg_wa", [4, 64, 64])
    rg_ba = din("rg_ba", [256])
    rg_wx = din("rg_wx", [4, 64, 64])
    rg_bx = din("rg_bx", [256])
    rg_lambda = din("rg_lambda", [256])
    w_mem_kv = din("w_mem_kv", [D, 512])

    y_prompt = dout("y_prompt", [T, D])
    y_sample = dout("y_sample", [DB, D])
    o_swa_k_p = dout("swa_k_p", [128, 128])
    o_swa_v_p = dout("swa_v_p", [128, 128])
    o_mem_k_p = dout("mem_k_p", [256, 256])
    o_mem_v_p = dout("mem_v_p", [256, 256])
    o_conv_p = dout("conv_p", [3, 256])
    o_h_p = dout("h_p", [256])
    o_swa_k_s = dout("swa_k_s", [DB, 128, 128])
    o_swa_v_s = dout("swa_v_s", [DB, 128, 128])
    o_conv_s = dout("conv_s", [DB, 3, 256])
    o_h_s = dout("h_s", [DB, 256])

    P = Prog(nc)

    def sb(name, shape, dt=F32):
        return nc.alloc_sbuf_tensor(name, shape, dt)

    stage = [sb("stage%d" % i, [128, 4096], F32) for i in range(2)]
    stage_ring = Ring("stage", stage)
    wslab = [sb("wslab%d" % i, [128, 8, 512], BF16) for i in range(2)]
    wslab_ring = Ring("wslab", wslab)
    wsr = [wslab_ring]
    claimed = set()

    def wslab_claim(wsn):
        if wsn[0] == "wslabx" and wsn not in claimed:
            claimed.add(wsn)
            return [("stage", wsn[1] // 2)]
        return []
    wbig = sb("wbig", [128, NFC * 1024], BF16)
    x_res = sb("x_res", [128, 4, D], F32)
    xn_ring = Ring("xn", [sb("xn%d" % i, [128, D], BF16) for i in range(2)])
    xnT = sb("xnT", [128, 8, NT], BF16)
    gT = sb("gT", [128, NFC, NT], BF16)
    tmp_ring = Ring("tmp", [sb("tmp%d" % i, [128, D], F32) for i in range(1)])
    x_s = sb("x_s", [DB, D], F32)
    xnT_s = sb("xnT_s", [128, 8, DB], BF16)
    gT_s = sb("gT_s", [128, NFC, DB], BF16)
    gpost = [sb("gpost%d" % i, [128, D], F32) for i in range(3)]
    gpre = sb("gpre", [128, 4, 8], F32)
    junk = sb("junk", [128, D], BF16)
    small = sb("small", [128, 64], F32)
    small_i = [0]

    def small_next(tag):
        k = small_i[0] % 64
        small_i[0] += 1
        return small[:, k:k + 1], ("small", k)

    ident = sb("ident", [128, 128], BF16)
    neg_half = sb("neg_half", [128, 1], F32)
    eps_c = sb("eps_c", [128, 2], F32)
    gflat = gT[:].rearrange("p a b -> p (a b)")
    qT = gflat[:, 0:2048].rearrange("p (a b) -> p a b", a=4)
    kdupT = gflat[:, 2048:2048 + 1280].rearrange("p (a b) -> p a b", a=2)
    qmT = gflat[:, 3584:3584 + 1024].rearrange("p (a b) -> p a b", a=2)
    omixT = gflat[:, 4608:4608 + 4096].rearrange("p (a b) -> p a b", a=8)
    pt_ring = Ring("gT", [gflat[:, (17 + i) * 512: (18 + i) * 512] for i in range(5)], off=17)
    QN = lambda c: ("gT", c)
    KDN = [("gT", 4), ("gT", 5), ("gT", 6)]
    QMN = lambda c: ("gT", 7 + c)
    OMN = lambda i: ("gT", 9 + i)
    kd_hist = sb("kd_hist", [128, 2, 128], BF16)
    vaug = sb("vaug", [128, 5, 2, 2, 128], BF16)
    xr = sb("xr", [128, 2, 3 + NT], F32)
    hbuf = sb("hbuf", [128, 2, 1 + NT], F32)
    rgbuf = sb("rgbuf", [128, 2, NT], F32)
    rg_u = rgbuf[:, 0, :]
    rg_r = rgbuf[:, 1, :]
    memkv = rgbuf
    MEMKVN = ["rg_u", "rg_r"]
    rg_ub = sb("rg_ub", [128, NT], BF16)
    rg_i = sb("rg_i", [128, NT], F32)
    rg_a = sb("rg_a", [128, NT], F32)
    rg_g = sb("rg_g", [128, 2, NT], F32)
    rd_ring = Ring("rd", [sb("rd%d" % i, [128, NT], F32) for i in range(2)])
    mask_po = sb("mask_po", [128, 512], BF16)
    sink_full = sb("sink_full", [128, 1024], BF16)
    e_sel = sb("e_sel", [128, 2, 128], BF16)
    sink_raw = sb("sink_raw", [1, 8], F32)
    sink_exp = sb("sink_exp", [1, 8], F32)
    convw = sb("convw", [128, 2, 4], F32)
    rgc = sb("rgc", [128, 8, 2], F32)
    wbd = sb("wbd", [128, 2, 2, 128], BF16)
    mkT = sb("mkT", [128, 2, 256], BF16)
    mvaug = sb("mvaug", [128, 2, 4, 128], BF16)
    kvout = sb("kvout", [128, 256], F32)
    vxaug = sb("vxaug", [1, 2, 512], BF16)

    ps = nc.alloc_psum_tensor("ps", [128, 6, 512], F32)
    psb = nc.alloc_psum_tensor("psb", [128, 2, 1024], BF16)
    bank_i = [0]
    pair_i = [0]
    psb_i = [0]

    pending_banks = set()

    def bank():
        while True:
            k = bank_i[0] % 6
            bank_i[0] += 1
            if k not in pending_banks:
                return k

    def pair():
        k = (pair_i[0] % 3) * 2
        pair_i[0] += 1
        return k

    def psb_next():
        k = psb_i[0] % 2
        psb_i[0] += 1
        return k

    PSN = lambda k: ("ps", k)

    def dma_in(dst, src, names, key):
        P.add("sp", lambda h: h.dma_start(out=dst, in_=src), writes=names, dma_key=key)

    def dma_in_nc(dst, src, names, key):
        P.add("sp", lambda h: h.dma_start(out=dst, in_=src, allow_slow_non_contiguous=True), writes=names, dma_key=key)

    def dma_cst(dst, src, names, nonc=False):
        if nonc:
            P.add("act", lambda h: h.dma_start(out=dst, in_=src, allow_slow_non_contiguous=True), writes=list(names) + ["cst_chain"], dma_key="cst")
        else:
            P.add("act", lambda h: h.dma_start(out=dst, in_=src), writes=list(names) + ["cst_chain"], dma_key="cst")

    def dma_out(dst, src, names, key, nonc=False):
        if nonc:
            P.add("sp", lambda h: h.dma_start(out=dst, in_=src, allow_slow_non_contiguous=True), reads=names, dma_key=key, final=True)
        else:
            P.add("sp", lambda h: h.dma_start(out=dst, in_=src), reads=names, dma_key=key, final=True)

    def mm(out, lhsT, rhs, start, stop, reads, writes, sgc=False):
        if sgc:
            P.add("pe", lambda h: h.matmul(out, lhsT=lhsT, rhs=rhs, start=start, stop=stop, skip_group_check=True), reads=reads, writes=writes)
        else:
            P.add("pe", lambda h: h.matmul(out, lhsT=lhsT, rhs=rhs, start=start, stop=stop), reads=reads, writes=writes)

    def act(out, in_, func, reads, writes, scale=None, bias=None, accum=None):
        kw = {}
        if scale is not None:
            kw["scale"] = scale
        if bias is not None:
            kw["bias"] = bias
        if accum is not None:
            kw["accum_out"] = accum
        P.add("act", lambda h: h.activation(out=out, in_=in_, func=func, **kw), reads=reads, writes=writes)

    def tt(eng, out, in0, in1, op, reads, writes):
        P.add(eng, lambda h: h.tensor_tensor(out=out, in0=in0, in1=in1, op=op), reads=reads, writes=writes)

    def ts(eng, out, in0, s1, s2, op0, op1, reads, writes):
        if op1 is None:
            P.add(eng, lambda h: h.tensor_scalar(out=out, in0=in0, scalar1=s1, scalar2=None, op0=op0), reads=reads, writes=writes)
        else:
            P.add(eng, lambda h: h.tensor_scalar(out=out, in0=in0, scalar1=s1, scalar2=s2, op0=op0, op1=op1), reads=reads, writes=writes)

    def stt(out, in0, scalar, in1, op0, op1, reads, writes):
        P.add("dve", lambda h: h.scalar_tensor_tensor(out=out, in0=in0, scalar=scalar, in1=in1, op0=op0, op1=op1), reads=reads, writes=writes)

    def copy(eng, out, in_, reads, writes):
        P.add(eng, lambda h: h.tensor_copy(out=out, in_=in_), reads=reads, writes=writes)

    def memset(eng, ap, val, writes):
        P.add(eng, lambda h: h.memset(ap, val), writes=writes)

    def recip(out, in_, reads, writes):
        P.add("dve", lambda h: h.reciprocal(out=out, in_=in_), reads=reads, writes=writes)

    cst_i = [0]

    def cst_key():
        cst_i[0] += 1
        return "cst%d" % cst_i[0]

    ones_f = stage[1][:, 0:512]
    memset("pool", ones_f, 1.0, ["ones_f", ("stage", 1)])
    memset("pool", neg_half[:], -0.5, ["neg_half"])
    memset("pool", eps_c[:, 0:1], EPS, ["eps_c"])
    memset("pool", eps_c[:, 1:2], 4.0 * EPS, ["eps_c"])
    P.add("pool", lambda h: h.affine_select(out=ident[:], in_=ones_f[:, 0:128], pattern=[[-1, 128]],
                                            compare_op=ALU.is_equal, fill=0.0, base=0, channel_multiplier=1),
          reads=["ones_f", ("stage", 1)], writes=["ident"])
    zeros_f = stage[1][:, 512:768]
    memset("pool", zeros_f, 0.0, ["zeros_f", ("stage", 1)])
    P.add("pool", lambda h: h.affine_select(out=mask_po[:, 0:256], in_=zeros_f, pattern=[[0, 2], [-1, 128]],
                                            compare_op=ALU.is_ge, fill=-30000.0, base=0, channel_multiplier=1),
          reads=["zeros_f", ("stage", 1)], writes=["mask_po"])
    P.add("pool", lambda h: h.affine_select(out=mask_po[:, 256:512], in_=zeros_f, pattern=[[0, 2], [1, 128]],
                                            compare_op=ALU.is_ge, fill=-30000.0, base=0, channel_multiplier=-1),
          reads=["zeros_f", ("stage", 1), "mask_po"], writes=["mask_po"])
    for i in range(4):
        dma_cst(gpre[:, i, :], ln_pre[i].rearrange("(k p) -> p k", p=128), [("gpre", i)], nonc=True)
    for i in range(3):
        dma_cst(gpost[i][:], ln_post[i].partition_broadcast(128), [("gpost", i)])
    for c in range(2):
        dma_cst(convw[:, c, :], conv_w[:, c * 128:(c + 1) * 128].rearrange("j p -> p j"), [("convw", c)], nonc=True)
    for i, src in enumerate([conv_b, rg_ba, rg_bx, rg_lambda]):
        dma_cst(rgc[:, i, :], src.rearrange("(c p) -> p c", p=128), [("rgc", i)], nonc=True)
    act(rgc[:, 6, :], rgc[:, 3, :], AF.Exp, [("rgc", 3)], [("rgc", 6)], scale=-1.0)
    act(rgc[:, 7, :], rgc[:, 6, :], AF.Ln, [("rgc", 6)], [("rgc", 7)], bias=1.0)
    ts("dve", rgc[:, 4, :], rgc[:, 7, :], -8.0, None, ALU.mult, None, [("rgc", 7)], [("rgc", 4)])
    ts("dve", rgc[:, 5, :], rgc[:, 7, :], -16.0, None, ALU.mult, None, [("rgc", 7)], [("rgc", 5)])
    memset("pool", sink_full[:], 0.0, ["sink_full"])
    dma_cst(sink_raw[:], swa_sinks.rearrange("(a h) -> a h", a=1), ["sink_raw"])
    act(sink_exp[:], sink_raw[:], AF.Exp, ["sink_raw"], ["sink_exp"])
    for kv in range(2):
        src = bass.AP(sink_exp, 4 * kv, [[8, 1], [1, 2], [2, 2], [0, 128]])
        dst = sink_full[0:1, kv * 512:(kv + 1) * 512].rearrange("o (p j q) -> o p j q", p=2, j=2)
        copy("dve", dst, src, ["sink_exp", "sink_full"], ["sink_full"])
    memset("pool", e_sel[:], 0.0, ["e_sel"])
    memset("pool", e_sel[0:1, 0, 64:128], 1.0, ["e_sel"])
    memset("pool", e_sel[0:1, 1, 0:64], 1.0, ["e_sel"])
    memset("pool", vaug[:], 1.0, ["vaug_init"])
    memset("pool", mvaug[:], 1.0, ["mvaug"])
    memset("pool", xr[:, :, 0:3], 0.0, ["xr_hist"])
    memset("pool", hbuf[:, :, 0:1], 0.0, ["h_carry"])
    memset("pool", kd_hist[:], 0.0, ["kd_hist"])
    st0 = stage[0]
    stv = st0[:, 0:512].rearrange("p (g c e) -> p g c e", g=2, c=2)
    memset("pool", st0[:, 0:512], 0.0, [("stage", 0)])
    for gi, wsrc in enumerate([rg_wa, rg_wx]):
        for c in range(2):
            for l in range(2):
                dma_cst(stv[l * 64:(l + 1) * 64, gi, c, l * 64:(l + 1) * 64], wsrc[2 * c + l], [("stage", 0)])
    copy("pool", wbd[:], stv, [("stage", 0)], ["wbd"])
    stage_ring.i = 1

    def load_piece(src_ap, a, b, dst_ap, dst_names, reads_extra=()):
        st, st_name = stage_ring.next()
        stv_ = st[:, 0:a * b].rearrange("p (a b) -> p a b", a=a)
        P.add("sp", lambda h: h.dma_start(out=stv_, in_=src_ap), writes=[st_name], dma_key=st_name)
        P.add("pool", lambda h: h.tensor_copy(out=dst_ap, in_=stv_), reads=[st_name] + list(reads_extra), writes=dst_names)

    scratch = {}
    pending = []
    cast_i = [0]
    store_i = [0]
    CAST_ENGS = ["act", "dve", "pool", "act", "dve"]

    def cast_op(dst_ap, src_ap, reads, writes):
        e = CAST_ENGS[cast_i[0] % len(CAST_ENGS)]
        cast_i[0] += 1
        if e == "act":
            act(dst_ap, src_ap, AF.Copy, reads, writes)
        else:
            copy(e, dst_ap, src_ap, reads, writes)

    def flush_stores(keep=0):
        while len(pending) > keep:
            pid, dst_ap, dst_names = pending.pop(0)
            k = store_i[0] % 6
            store_i[0] += 1
            P.add("sp", (lambda pid, dst_ap: lambda h: h.dma_start(out=scratch[pid], in_=dst_ap))(pid, dst_ap),
                  reads=dst_names, writes=[("scr", pid), ("scrkey", k)], dma_key=("scrkey", k))

    def cached_load(pid, dst_ap, dst_names):
        P.add("sp", lambda h: h.dma_start(out=dst_ap, in_=scratch[pid]), reads=[("scr", pid)], writes=dst_names,
              dma_key=("ld",) + tuple(dst_names[0]))

    def piece(pid, src_ap, a, b, dst_ap, dst_names, post_cast=None, first_names=None):
        if pid not in scratch:
            if first_names is not None:
                dst_names = first_names
            scratch[pid] = nc.dram_tensor("scr_" + pid, [128, a, b], BF16).ap()
            st, st_name = stage_ring.next()
            stv_ = st[:, 0:a * b].rearrange("p (a b) -> p a b", a=a)
            P.add("sp", lambda h: h.dma_start(out=stv_, in_=src_ap), writes=[st_name], dma_key=st_name)
            cast_op(dst_ap, stv_, [st_name], dst_names)
            if post_cast is not None:
                post_cast(stv_, st_name)
            pending.append((pid, dst_ap, dst_names))
            flush_stores(keep=1)
        else:
            cached_load(pid, dst_ap, dst_names)

    def rstd_from_ss(ss_ap, ss_name, c1, c2, pn):
        v, vn = small_next("v")
        r, rn = small_next("r")
        ts("dve", v[0:pn], ss_ap[0:pn], c1, c2, ALU.mult, ALU.add, [ss_name], [vn]) if False else None
        act(v[0:pn], ss_ap[0:pn], AF.Ln, [ss_name, "eps_c"], [vn], scale=c1, bias=(eps_c[0:pn, 0:1] if c2 == EPS else eps_c[0:pn, 1:2]))
        act(r[0:pn], v[0:pn], AF.Exp, [vn], [rn], scale=-0.5)
        return r, rn

    def prenorm_a(src, src_name, pn):
        ss, ssn = small_next("ss")
        act(junk[0:pn], src, AF.Square, [src_name], [ssn, "junk"], accum=ss[0:pn])
        r, rn = rstd_from_ss(ss, ssn, 1.0 / D, EPS, pn)
        xn, xnn = xn_ring.next()
        act(xn[0:pn], src, AF.Copy, [src_name, rn], [xnn], scale=r[0:pn])
        return xn, xnn

    def prenorm_sub(s, src, src_name, pn, gi, dstT, dst_name, pre=None):
        xn, xnn = pre if pre is not None else prenorm_a(src, src_name, pn)
        k = psb_next()
        for kc in range(8):
            P.add("pe", (lambda k, kc, xn: lambda h: h.transpose(out=psb[:, k, kc * 128:kc * 128 + pn], in_=xn[0:pn, kc * 128:(kc + 1) * 128],
                                                               identity=ident[0:pn, 0:pn]))(k, kc, xn),
                  reads=[xnn, "ident"], writes=[("psb", k)])
        fine = tuple(("xk", dst_name, kc) for kc in range(8))
        P.expand[dst_name] = fine
        for kc in range(8):
            ts("dve", dstT[:, kc, s * 128:s * 128 + pn], psb[:, k, kc * 128:kc * 128 + pn], gpre[:, gi, kc:kc + 1], None, ALU.mult, None,
               [("psb", k), ("gpre", gi)], [fine[kc]])

    def prenorm_transpose(src_sub, src_names, nsub, pn, gi, dstT, dst_name_fn):
        for s in range(nsub):
            prenorm_sub(s, src_sub(s), src_names[s], pn, gi, dstT, dst_name_fn(s))

    deferred = []

    def run_deferred():
        while deferred:
            deferred.pop(0)()

    def out_loop(nsub, pn, group, gi_post, half_factor, after_sub):
        for s in range(nsub):
            pk = group(s)
            postnorm_residual(pk, s, pn, gi_post, half_factor)
            if after_sub is not None:
                if s >= 2:
                    after_sub[1](s - 2)
                after_sub[0](s)
        if after_sub is not None:
            for s2 in range(max(0, nsub - 2), nsub - 1):
                after_sub[1](s2)
            deferred.append((lambda f, s_: lambda: f(s_))(after_sub[1], nsub - 1))

    def postnorm_residual(pk, s, pn, gi, half_factor, xbuf=None, xname=None):
        if xbuf is None:
            xbuf, xname = x_res[0:pn, s, :], ("x", s)
        y = ps[0:pn, pk:pk + 2, :]
        ss, ssn = small_next("ss")
        act(junk[0:pn].rearrange("p (a b) -> p a b", a=2), y, AF.Square, [PSN(pk), PSN(pk + 1)], [ssn, "junk"], accum=ss[0:pn])
        tmp, tmpn = tmp_ring.next()
        tt("dve", tmp[0:pn].rearrange("p (a b) -> p a b", a=2), y, gpost[gi][0:pn].rearrange("p (a b) -> p a b", a=2), ALU.mult,
           [PSN(pk), PSN(pk + 1), ("gpost", gi)], [tmpn])
        if half_factor:
            r, rn = rstd_from_ss(ss, ssn, 4.0 / D, 4.0 * EPS, pn)
        else:
            r, rn = rstd_from_ss(ss, ssn, 1.0 / D, EPS, pn)
        stt(xbuf, tmp[0:pn], r[0:pn], xbuf, ALU.mult, ALU.add, [tmpn, rn, xname], [xname])

    prefetched = {}
    qslot = [None]

    def prefetch_q():
        if cached_mode[0] and "wmi0" in scratch:
            wsl, wsn = wsr[0].next()
            cached_load("wmi0", wsl[:, :, :], [wsn] + wslab_claim(wsn))
            qslot[0] = (wsl, wsn)
    cached_mode = [False]

    def slab_src(fi, j, part):
        c0 = j * 512
        w = min(512, DFF - c0)
        return w_ffn_in[fi][:, part * DFF + c0: part * DFF + c0 + w].rearrange("(k p) f -> p k f", p=128), w

    def load_slab(fi, j, part):
        src, w = slab_src(fi, j, part)
        wsl, wsn = wsr[0].next()
        piece("f%ds%dp%d" % (fi, j, part), src, 8, w, wsl[:, :, 0:w], [wsn] + wslab_claim(wsn))
        return wsl, wsn

    def prefetch_slabs(fi, count):
        order = [(j, part) for j in range(6) for part in range(2)]
        for (j, part) in order[:count]:
            prefetched[(fi, j, part)] = load_slab(fi, j, part)

    def ffn(fi, gi_pre, gi_post, n, nsub, pn, do_pre=True, after_sub=None, before_out=None, ride=False):
        if ride:
            prenorm_sub(0, x_s[:], "x_s", DB, gi_pre, xnT_s, "xnT_s")
        if do_pre:
            prenorm_transpose(lambda s: x_res[0:pn, s, :], [("x", s) for s in range(nsub)], nsub, pn, gi_pre, xnT,
                              lambda s: ("xnT", s))
        xnT_names = [("xnT", s) for s in range(nsub)]
        wout = w_ffn_out[fi]
        split = cached_mode[0] and n == NT and len(deferred) > 0
        if split:
            slabs0 = {}
            for part in range(2):
                slabs0[part] = prefetched.pop((fi, 0, part)) if (fi, 0, part) in prefetched else load_slab(fi, 0, part)
            for (c0, c1, names) in ((0, 384, xnT_names[0:3]), (384, 512, xnT_names[3:4])):
                if c0 == 384:
                    run_deferred()
                for part in range(2):
                    wsl, wsn = slabs0[part]
                    for fl in range(4):
                        fc = fl
                        b = bank()
                        for kc in range(8):
                            mm(ps[:, b, 0:c1 - c0], wsl[:, kc, fl * 128:(fl + 1) * 128], xnT[:, kc, c0:c1], kc == 0, kc == 7, [wsn] + names, [PSN(b)])
                        if part == 0:
                            act(gT[:, fc, c0:c1], ps[:, b, 0:c1 - c0], AF.Silu, [PSN(b)], [("gT", fc)])
                        else:
                            tt("dve", gT[:, fc, c0:c1], ps[:, b, 0:c1 - c0], gT[:, fc, c0:c1], ALU.mult, [PSN(b), ("gT", fc)], [("gT", fc)])
                        if ride and c0 == 384:
                            b = bank()
                            for kc in range(8):
                                mm(ps[:, b, 0:DB], wsl[:, kc, fl * 128:(fl + 1) * 128], xnT_s[:, kc, :], kc == 0, kc == 7, [wsn, "xnT_s"], [PSN(b)])
                            if part == 0:
                                act(gT_s[:, fc, :], ps[:, b, 0:DB], AF.Silu, [PSN(b)], [("gT_s", fc)])
                            else:
                                tt("dve", gT_s[:, fc, :], ps[:, b, 0:DB], gT_s[:, fc, :], ALU.mult, [PSN(b), ("gT_s", fc)], [("gT_s", fc)])
        else:
            run_deferred()
        for j in range(6):
            if split and j == 0:
                continue
            w = min(512, DFF - j * 512)
            nch = w // 128
            for part in range(2):
                if (fi, j, part) in prefetched:
                    wsl, wsn = prefetched.pop((fi, j, part))
                else:
                    wsl, wsn = load_slab(fi, j, part)
                for fl in range(nch):
                    fc = j * 4 + fl
                    b = bank()
                    for kc in range(8):
                        mm(ps[:, b, 0:n], wsl[:, kc, fl * 128:(fl + 1) * 128], xnT[:, kc, 0:n], kc == 0, kc == 7,
                           [wsn] + xnT_names, [PSN(b)])
                    if part == 0:
                        act(gT[:, fc, 0:n], ps[:, b, 0:n], AF.Silu, [PSN(b)], [("gT", fc)])
                    else:
                        tt("dve", gT[:, fc, 0:n], ps[:, b, 0:n], gT[:, fc, 0:n], ALU.mult, [PSN(b), ("gT", fc)], [("gT", fc)])
                    if ride:
                        b = bank()
                        for kc in range(8):
                            mm(ps[:, b, 0:DB], wsl[:, kc, fl * 128:(fl + 1) * 128], xnT_s[:, kc, :], kc == 0, kc == 7, [wsn, "xnT_s"], [PSN(b)])
                        if part == 0:
                            act(gT_s[:, fc, :], ps[:, b, 0:DB], AF.Silu, [PSN(b)], [("gT_s", fc)])
                        else:
                            tt("dve", gT_s[:, fc, :], ps[:, b, 0:DB], gT_s[:, fc, :], ALU.mult, [PSN(b), ("gT_s", fc)], [("gT_s", fc)])
        wv = wbig[:].rearrange("p (f d) -> p f d", f=NFC)
        for pc in range(6):
            f0 = pc * 4
            nf = min(4, NFC - f0)
            src = wout[f0 * 128:(f0 + nf) * 128, :].rearrange("(f p) d -> p f d", p=128)
            piece("f%do%d" % (fi, pc), src, nf, D, wv[:, f0:f0 + nf, :], [("wbig", f0 + i) for i in range(nf)])
        if before_out is not None:
            before_out()

        def group(s):
            pk = pair()
            for half in range(2):
                for fc in range(NFC):
                    mm(ps[0:pn, pk + half, :], gT[:, fc, s * 128:s * 128 + pn], wv[:, fc, half * 512:(half + 1) * 512],
                       fc == 0, fc == NFC - 1, [("gT", fc), ("wbig", fc)], [PSN(pk + half)])
            return pk
        out_loop(nsub, pn, group, gi_post, True, after_sub)
        if ride:
            pk = pair()
            for half in range(2):
                for fc in range(NFC):
                    mm(ps[0:DB, pk + half, :], gT_s[:, fc, :], wv[:, fc, half * 512:(half + 1) * 512],
                       fc == 0, fc == NFC - 1, [("gT_s", fc), ("wbig", fc)], [PSN(pk + half)])
            postnorm_residual(pk, 0, DB, gi_post, True, xbuf=x_s[:], xname="x_s")

    WI = 1536 + 256
    wmi = wbig[:, 0:8 * WI].rearrange("p (k f) -> p k f", k=8)
    wmo = wbig[:, 8 * WI:8 * WI + 8 * D].rearrange("p (k f) -> p k f", k=8)
    ALLBIG = [("wbig", i) for i in range(NFC)]

    REG_WMI = [("wbig", f) for f in range(14)]
    REG_WMO = [("wbig", f) for f in range(14, NFC)]

    def load_mix_weights(skip_q=False):
        KD_NAMES = [("wmi", 3 + i) for i in range(4)]
        first = "wkd" not in scratch
        for pc in range(3):
            if pc == 0 and skip_q:
                continue
            src = w_in[:, pc * 512:(pc + 1) * 512].rearrange("(k p) f -> p k f", p=128)
            post = None
            if pc == 1 and first:
                def post(stv_, st_name):
                    for kv in range(2):
                        for dpl in range(2):
                            cast_op(wmi[:, :, 1536 + kv * 128 + dpl * 64: 1536 + kv * 128 + (dpl + 1) * 64], stv_[:, :, kv * 64:(kv + 1) * 64],
                                    [st_name], [("wmi", 3 + kv * 2 + dpl)] + REG_WMI)
            piece("wmi%d" % pc, src, 8, 512, wmi[:, :, pc * 512:(pc + 1) * 512],
                  ALLBIG if pc == 0 else ((ALLBIG + [("wmi", pc)]) if (pc == 1 and skip_q) else [("wmi", pc)]), post_cast=post,
                  first_names=(ALLBIG if pc == 0 else [("wmi", pc)] + REG_WMI))
            if pc == 1:
                if first:
                    scratch["wkd"] = nc.dram_tensor("scr_wkd", [128, 8, 256], BF16).ap()
                    pending.append(("wkd", wmi[:, :, 1536:1792], KD_NAMES + REG_WMI))
                else:
                    cached_load("wkd", wmi[:, :, 1536:1792], KD_NAMES)
        for pc in range(2):
            src = w_out[:, pc * 512:(pc + 1) * 512].rearrange("(k p) f -> p k f", p=128)
            piece("wmo%d" % pc, src, 8, 512, wmo[:, :, pc * 512:(pc + 1) * 512], [("wmo", pc)], first_names=[("wmo", pc)] + REG_WMO)
    WMI_ALL = ALLBIG + [("wmi", i) for i in range(1, 7)]

    def mem_kv_prompt():
        mtile = [x_res[:, 0, :], x_res[:, 1, :]]
        for s in range(2):
            dma_in(mtile[s], mem_prompt[s * 128:(s + 1) * 128, :], [("x", s)], "memld%d" % s)
        prenorm_transpose(lambda s: mtile[s], [("x", 0), ("x", 1)], 2, 128, 3, xnT, lambda s: ("xnT", s))
        wsl, wsn = wsr[0].next()
        load_piece(w_mem_kv.rearrange("(k p) f -> p k f", p=128), 8, 512, wsl[:], [wsn])
        names = [("xnT", 0), ("xnT", 1), wsn]
        for s in range(2):
            b = bank()
            for kc in range(8):
                mm(ps[:, b, :], xnT[:, kc, s * 128:(s + 1) * 128], wsl[:, kc, :], kc == 0, kc == 7, names, [PSN(b)])
            act(memkv[:, s, :], ps[:, b, :], AF.Copy, [PSN(b)], [MEMKVN[s]])
            dma_out(o_mem_k_p[s * 128:(s + 1) * 128, :], memkv[:, s, 0:256], [MEMKVN[s]], "st_memk%d" % s)
            dma_out(o_mem_v_p[s * 128:(s + 1) * 128, :], memkv[:, s, 256:512], [MEMKVN[s]], "st_memv%d" % s)
            for mh in range(4):
                pm = mh % 2
                cols = slice(0, 64) if pm == 0 else slice(64, 128)
                copy("dve", mvaug[:, s, mh, cols], memkv[:, s, 256 + mh * 64: 256 + (mh + 1) * 64], [MEMKVN[s], "mvaug"], ["mvaug"])
        for cm in range(2):
            b = bank()
            for kc in range(8):
                mm(ps[:, b, 0:256], wsl[:, kc, cm * 128:(cm + 1) * 128], xnT[:, kc, 0:256], kc == 0, kc == 7, names, [PSN(b)])
            act(mkT[:, cm, :], ps[:, b, 0:256], AF.Copy, [PSN(b)], ["mkT"])

    def mixer_prompt(ti, last, do_pre=True, after_sub=None, at_start=None):
        run_deferred()
        n = NT
        if do_pre:
            prenorm_transpose(lambda s: x_res[:, s, :], [("x", s) for s in range(4)], 4, 128, 1, xnT, lambda s: ("xnT", s))
        xnT_names = [("xnT", s) for s in range(4)]
        qs = qslot[0]
        qslot[0] = None
        load_mix_weights(skip_q=(qs is not None and not (last and do_sample)))
        if at_start is not None:
            at_start()

        def proj(col0, evac):
            b = bank()
            for kc in range(8):
                mm(ps[:, b, 0:n], wmi[:, kc, col0:col0 + 128], xnT[:, kc, 0:n], kc == 0, kc == 7, WMI_ALL + xnT_names, [PSN(b)])
            evac(b)
        for c in range(4):
            if qs is None:
                proj(c * 128, lambda b, c=c: act(qT[:, c, :], ps[:, b, :], AF.Copy, [PSN(b)], [QN(c)], scale=0.125))
            else:
                b = bank()
                for kc in range(8):
                    mm(ps[:, b, 0:n], qs[0][:, kc, c * 128:(c + 1) * 128], xnT[:, kc, 0:n], kc == 0, kc == 7, [qs[1]] + xnT_names, [PSN(b)])
                act(qT[:, c, :], ps[:, b, :], AF.Copy, [PSN(b)], [QN(c)], scale=0.125)
        for kv in range(2):
            proj(1536 + kv * 128, lambda b, kv=kv: act(kdupT[:, kv, 128:640], ps[:, b, :], AF.Copy, [PSN(b)], KDN))
        for c in range(2):
            proj(768 + c * 128, lambda b, c=c: act(qmT[:, c, :], ps[:, b, :], AF.Copy, [PSN(b)], [QMN(c)], scale=0.125))
        for c in range(2):
            proj(1024 + c * 128, lambda b, c=c: copy("dve", xr[:, c, 3:3 + n], ps[:, b, :], [PSN(b)], [("xr", c)]))
        for c in range(2):
            proj(1280 + c * 128, lambda b, c=c: act(rg_g[:, c, :], ps[:, b, :], AF.Gelu_apprx_tanh, [PSN(b)], [("rg_g", c)]))
        for s in range(4):
            g = ti * 4 + s
            slot = g % 5
            b = bank()
            fin = last and s == 3
            c0, c1 = (512, 768) if fin else (640, 768)
            for kc in range(8):
                mm(ps[:, b, 0:c1 - c0], xnT[:, kc, s * 128:(s + 1) * 128], wmi[:, kc, c0:c1], kc == 0, kc == 7,
                   WMI_ALL + xnT_names, [PSN(b)])
            voff = 128 if fin else 0
            for kv in range(2):
                act(vaug[:, slot, kv, 0, 0:64], ps[:, b, voff + kv * 64: voff + (kv + 1) * 64], AF.Copy, [PSN(b), "vaug_init"], [("vaug", slot)])
                act(vaug[:, slot, kv, 1, 64:128], ps[:, b, voff + kv * 64: voff + (kv + 1) * 64], AF.Copy, [PSN(b), "vaug_init"], [("vaug", slot)])
            if fin:
                act(kvout[:], ps[:, b, 0:256], AF.Copy, [PSN(b)], ["kvout"])
                dma_out(o_swa_k_p, kvout[:, 0:128], ["kvout"], "st_swak")
                dma_out(o_swa_v_p, kvout[:, 128:256], ["kvout"], "st_swav")
        copy("pool", kdupT[:, :, 0:128], kd_hist[:], ["kd_hist"], KDN)
        units = []

        def swa_unit(s, kv):
            g = ti * 4 + s
            st = {}

            def qk():
                st["banks"] = [bank(), bank()]
                pending_banks.update(st["banks"])
                cs = slice(0, 512) if g > 0 else slice(256, 512)
                for p in range(2):
                    mm(ps[:, st["banks"][p], cs], ident[:], mask_po[:, cs], True, False, ["ident", "mask_po"], [PSN(st["banks"][p])], sgc=True)
                for which in (["prev", "own"] if g > 0 else ["own"]):
                    kc0 = s * 128 if which == "prev" else 128 + s * 128
                    c0 = 0 if which == "prev" else 256
                    for p in range(2):
                        b = st["banks"][p]
                        mm(ps[:, b, c0:c0 + 256].rearrange("k (j q) -> k j q", j=2),
                           kdupT[p * 64:(p + 1) * 64, kv, kc0:kc0 + 128],
                           qT[p * 64:(p + 1) * 64, 2 * kv:2 * kv + 2, s * 128:(s + 1) * 128], False, which == "own",
                           KDN + [QN(2 * kv), QN(2 * kv + 1)], [PSN(b)], sgc=True)

            def ex():
                pending_banks.difference_update(st["banks"])
                st["pt"] = []
                cs = slice(0, 512) if g > 0 else slice(256, 512)
                for p in range(2):
                    b = st["banks"][p]
                    pt, ptn = pt_ring.next()
                    act(pt[:, cs], ps[:, b, cs], AF.Exp, [PSN(b)], [ptn])
                    st["pt"].append((pt, ptn))

            def pv():
                b = bank()
                st["bv"] = b
                for p in range(2):
                    cols = slice(p * 256, (p + 1) * 256)
                    pt, ptn = st["pt"][p]
                    if g > 0:
                        mm(ps[:, b, cols], vaug[:, (g - 1) % 5, kv, p, :], pt[:, 0:256], True, False,
                           [("vaug", (g - 1) % 5), ptn, "vaug_init"], [PSN(b)])
                    mm(ps[:, b, cols], vaug[:, g % 5, kv, p, :], pt[:, 256:512], g == 0, False,
                       [("vaug", g % 5), ptn, "vaug_init"], [PSN(b)])
                    mm(ps[:, b, cols], e_sel[:, p, :], sink_full[:, (kv * 2 + p) * 256:(kv * 2 + p + 1) * 256], False, True,
                       ["e_sel", "sink_full"], [PSN(b)])

            def nrm():
                b = st["bv"]
                rd, rdn = rd_ring.next()
                recip(rd[0:64, 0:256], ps[64:128, b, 0:256], [PSN(b)], [rdn])
                recip(rd[64:128, 256:512], ps[0:64, b, 256:512], [PSN(b)], [rdn])
                tt("dve", omixT[0:64, 2 * kv:2 * kv + 2, s * 128:(s + 1) * 128],
                   ps[0:64, b, 0:256].rearrange("p (j q) -> p j q", j=2), rd[0:64, 0:256].rearrange("p (j q) -> p j q", j=2), ALU.mult,
                   [PSN(b), rdn], [OMN(2 * kv), OMN(2 * kv + 1)])
                tt("dve", omixT[64:128, 2 * kv:2 * kv + 2, s * 128:(s + 1) * 128],
                   ps[64:128, b, 256:512].rearrange("p (j q) -> p j q", j=2), rd[64:128, 256:512].rearrange("p (j q) -> p j q", j=2), ALU.mult,
                   [PSN(b), rdn], [OMN(2 * kv), OMN(2 * kv + 1)])
            return qk, ex, pv, nrm

        def mem_unit(mh):
            cm, pm = mh // 2, mh % 2
            prow = slice(pm * 64, (pm + 1) * 64)
            drow = slice((1 - pm) * 64, (2 - pm) * 64)
            st = {}

            def qk():
                st["banks"] = [bank(), bank()]
                pending_banks.update(st["banks"])
                for mc in range(2):
                    b = st["banks"][mc]
                    mm(ps[:, b, 0:n], mkT[prow, cm, mc * 128:(mc + 1) * 128], qmT[prow, cm, 0:n], True, True, ["mkT", QMN(cm)], [PSN(b)])

            def ex():
                pending_banks.difference_update(st["banks"])
                st["pt"] = []
                for mc in range(2):
                    b = st["banks"][mc]
                    pt, ptn = pt_ring.next()
                    act(pt, ps[:, b, :], AF.Exp, [PSN(b)], [ptn])
                    st["pt"].append((pt, ptn))

            def pv():
                b = bank()
                st["bv"] = b
                for mc in range(2):
                    pt, ptn = st["pt"][mc]
                    mm(ps[:, b, 0:n], mvaug[:, mc, mh, :], pt, mc == 0, mc == 1, ["mvaug", ptn], [PSN(b)])

            def nrm():
                b = st["bv"]
                rd, rdn = rd_ring.next()
                recip(rd[prow, :], ps[drow, b, :], [PSN(b)], [rdn])
                tt("dve", omixT[prow, 4 + cm, :], ps[prow, b, :], rd[prow, :], ALU.mult, [PSN(b), rdn], [OMN(4 + cm)])
            return qk, ex, pv, nrm

        for s in range(4):
            for kv in range(2):
                units.append(swa_unit(s, kv))
        for mh in range(4):
            units.append(mem_unit(mh))

        def rg_a_part(c):
            ts("dve", rg_u, xr[:, c, 0:n], convw[:, c, 0:1], rgc[:, 0, c:c + 1], ALU.mult, ALU.add,
               [("xr", c), "xr_hist", ("convw", c), ("rgc", 0)], ["rg_u"])
            for j in range(1, 4):
                stt(rg_u, xr[:, c, j:j + n], convw[:, c, j:j + 1], rg_u, ALU.mult, ALU.add,
                    [("xr", c), "xr_hist", ("convw", c), "rg_u"], ["rg_u"])
            act(rg_ub[:], rg_u, AF.Copy, ["rg_u"], ["rg_ub"])

        def rg_b_part(c):
            ba = bank()
            mm(ps[:, ba, :], wbd[:, 0, c, :], rg_ub[:], True, True, ["wbd", "rg_ub"], [PSN(ba)])
            bx = bank()
            mm(ps[:, bx, :], wbd[:, 1, c, :], rg_ub[:], True, True, ["wbd", "rg_ub"], [PSN(bx)])
            act(rg_r, ps[:, ba, :], AF.Sigmoid, [PSN(ba), ("rgc", 1)], ["rg_r"], bias=rgc[:, 1, c:c + 1])
            act(rg_i[:], ps[:, bx, :], AF.Sigmoid, [PSN(bx), ("rgc", 2)], ["rg_i"], bias=rgc[:, 2, c:c + 1])
            act(rg_a[:], rg_r, AF.Exp, ["rg_r", ("rgc", 4)], ["rg_a"], scale=rgc[:, 4, c:c + 1])
            act(rg_r, rg_r, AF.Exp, ["rg_r", ("rgc", 5)], ["rg_r"], scale=rgc[:, 5, c:c + 1])
            act(rg_r, rg_r, AF.Sqrt, ["rg_r"], ["rg_r"], scale=-1.0, bias=1.0)

        def rg_c_part(c):
            tt("dve", rg_i[:], rg_i[:], rg_u, ALU.mult, ["rg_i", "rg_u"], ["rg_i"])
            tt("dve", rg_i[:], rg_i[:], rg_r, ALU.mult, ["rg_i", "rg_r"], ["rg_i"])
            P.add("dve", (lambda c: lambda h: h.tensor_tensor_scan(out=hbuf[:, c, 1:1 + n], data0=rg_a[:], data1=rg_i[:],
                                                                  initial=hbuf[:, c, 0:1], op0=ALU.mult, op1=ALU.add))(c),
                  reads=["rg_a", "rg_i", "h_carry"], writes=[("h", c)])
            tt("dve", omixT[:, 6 + c, :], hbuf[:, c, 1:1 + n], rg_g[:, c, :], ALU.mult, [("h", c), ("rg_g", c)], [OMN(6 + c)])
        extra = {0: lambda: rg_a_part(0), 1: lambda: rg_b_part(0), 2: lambda: rg_c_part(0),
                 3: lambda: rg_a_part(1), 4: lambda: rg_b_part(1), 5: lambda: rg_c_part(1)}

        units[0][0]()
        units[1][0]()
        units[0][1]()
        for i, (qk, ex, pv, nrm) in enumerate(units):
            if i + 2 < len(units):
                units[i + 2][0]()
            if i + 1 < len(units):
                units[i + 1][1]()
            pv()
            nrm()
            if i in extra:
                extra[i]()
            if i == 5:
                copy("pool", kd_hist[:], kdupT[:, :, 512:640], KDN, ["kd_hist"])
        if last:
            for c in range(2):
                dma_out(o_conv_p[:, c * 128:(c + 1) * 128].rearrange("j p -> p j"), xr[:, c, n:n + 3], [("xr", c)], "st_convp%d" % c, nonc=True)
                dma_out(o_h_p[c * 128:(c + 1) * 128].rearrange("(p o) -> p o", o=1), hbuf[:, c, n:n + 1], [("h", c)], "st_hp%d" % c, nonc=True)
        else:
            copy("pool", xr[:, :, 0:3], xr[:, :, n:n + 3], [("xr", 0), ("xr", 1)], ["xr_hist"])
            copy("pool", hbuf[:, :, 0:1], hbuf[:, :, n:n + 1], [("h", 0), ("h", 1)], ["h_carry"])
        om_names = [OMN(i) for i in range(8)]

        def group(s):
            pk = pair()
            for half in range(2):
                for ch in range(8):
                    mm(ps[:, pk + half, :], omixT[:, ch, s * 128:(s + 1) * 128], wmo[:, ch, half * 512:(half + 1) * 512],
                       ch == 0, ch == 7, om_names + ALLBIG + [("wmo", 0), ("wmo", 1)], [PSN(pk + half)])
            return pk
        out_loop(4, 128, group, 1, False, after_sub)

    def mixer_sample(reload_weights=True):
        run_deferred()
        n = DB
        xnT = xnT_s
        prenorm_sub(0, x_s[:], "x_s", n, 1, xnT, "xnT_s")
        xnT_names = ["xnT_s"]
        if reload_weights:
            load_mix_weights()
        old_names = [("rg_g", 0), ("rg_g", 1), "rg_a", "rg_i", "rg_u", "rg_r", "rg_ub", "kd_hist", "mkT", "mvaug", "vaug_init", "mask_po", "junk",
                     "xr_hist", "h_carry", ("xr", 0), ("xr", 1), ("h", 0), ("h", 1), "kvout"] + [("vaug", i) for i in range(5)]
        ksf_ring = Ring("s_ksf", [rg_g[:, 0, 0:128], rg_g[:, 0, 128:256]])
        vsf_ring = Ring("s_vsf", [rg_g[:, 0, 256:384], rg_g[:, 0, 384:512]])
        mksf_ring = Ring("s_mksf", [rg_a[:].rearrange("p (m f) -> p m f", m=2), rg_u.rearrange("p (m f) -> p m f", m=2)])
        mvsf_ring = Ring("s_mvsf", [rg_i[:].rearrange("p (m f) -> p m f", m=2), rg_r.rearrange("p (m f) -> p m f", m=2)])
        kdup_ring = Ring("s_kdup", [rg_ub[:, 0:256].rearrange("p (k c) -> p k c", k=2), rg_ub[:, 256:512].rearrange("p (k c) -> p k c", k=2)])
        vx_ring = Ring("s_vx", [hbuf[0:1, 0, 0:128], hbuf[0:1, 0, 128:256]])
        vxa_ring = Ring("s_vxa", [vxaug[0:1, i, :].rearrange("o (k f c) -> o k f c", k=2, f=2) for i in range(2)])
        s_xr = xr[:, :, 0:n]
        s_cv = xr[:, :, 16:64].rearrange("p c (j b) -> p c j b", j=3)
        s_h0 = xr[:, :, 64:80]
        s_u = xr[:, :, 80:96]
        s_r = xr[:, :, 96:112]
        s_i = xr[:, :, 112:128]
        s_a = xr[:, :, 128:144]
        s_h = xr[:, :, 144:160]
        s_g = xr[:, :, 160:176]
        new_names = ([("s_ksf", i) for i in range(2)] + [("s_vsf", i) for i in range(2)] + [("s_mksf", i) for i in range(2)] +
                     [("s_mvsf", i) for i in range(2)] + [("s_kdup", i) for i in range(2)] + [("s_vx", i) for i in range(2)] +
                     [("s_vxa", i) for i in range(2)] + ["s_xr", "s_cv", "s_h0", "s_u", "s_r", "s_i", "s_a", "s_h", "s_g", "s_kvtok", "s_mkT", "s_mvaug"] +
                     [("s_vaug", i) for i in range(5)] +
                     [(nm, c) for nm in ("s_u", "s_r", "s_i", "s_a", "s_h", "s_ub") for c in range(2)])
        P.add("pool", lambda h: h.memset(vxaug[:], 1.0), writes=old_names + new_names)
        mkT_slots = [(mkT, "s_mkT"), (mask_po[:].rearrange("p (c m) -> p c m", c=2), "mask_po")]
        mvaug_slots = [(mvaug[:], "s_mvaug"), (junk[:].rearrange("p (m h c) -> p m h c", m=2, h=4), "junk")]
        P.add("pool", lambda h: h.memset(junk[:], 1.0), reads=["junk"], writes=["junk"])

        def proj(col0, evac):
            b = bank()
            for kc in range(8):
                mm(ps[:, b, 0:n], wmi[:, kc, col0:col0 + 128], xnT[:, kc, 0:n], kc == 0, kc == 7, WMI_ALL + xnT_names, [PSN(b)])
            evac(b)
        for c in range(4):
            proj(c * 128, lambda b, c=c: act(qT[:, c, 0:n], ps[:, b, 0:n], AF.Copy, [PSN(b)], [QN(c)], scale=0.125))
        for kv in range(2):
            proj(1536 + kv * 128, lambda b, kv=kv: act(kdupT[:, kv, 128:128 + n], ps[:, b, 0:n], AF.Copy, [PSN(b)], KDN))
        for c in range(2):
            proj(768 + c * 128, lambda b, c=c: act(qmT[:, c, 0:n], ps[:, b, 0:n], AF.Copy, [PSN(b)], [QMN(c)], scale=0.125))
        for c in range(2):
            proj(1024 + c * 128, lambda b, c=c: copy("dve", s_xr[:, c, :], ps[:, b, 0:n], [PSN(b)], ["s_xr"]))
        for c in range(2):
            proj(1280 + c * 128, lambda b, c=c: act(s_g[:, c, :], ps[:, b, 0:n], AF.Gelu_apprx_tanh, [PSN(b)], ["s_g"]))
        b = bank()
        for kc in range(8):
            mm(ps[0:n, b, 0:256], xnT[:, kc, 0:n], wmi[:, kc, 512:768], kc == 0, kc == 7, WMI_ALL + xnT_names, [PSN(b)])
        act(kvout[0:n, :], ps[0:n, b, 0:256], AF.Copy, [PSN(b)], ["s_kvtok"])
        P.add("sp", lambda h: h.dma_start(out=o_swa_k_s[:, 0:127, :], in_=cache_swa_k[:, 1:128, :]), dma_key="st_sk_shift", final=True)
        P.add("sp", lambda h: h.dma_start(out=o_swa_v_s[:, 0:127, :], in_=cache_swa_v[:, 1:128, :]), dma_key="st_sv_shift", final=True)
        dma_out(o_swa_k_s[:, 127, :], kvout[0:n, 0:128], ["s_kvtok"], "st_sk_new")
        dma_out(o_swa_v_s[:, 127, :], kvout[0:n, 128:256], ["s_kvtok"], "st_sv_new")
        P.add("sp", lambda h: h.dma_start(out=o_conv_s[:, 0:2, :], in_=state_conv[:, 1:3, :]), dma_key="st_cv_shift", final=True)
        for bi in range(n):
            ksf, ksn = ksf_ring.next()
            vsf, vsn = vsf_ring.next()
            mksf, mksn = mksf_ring.next()
            mvsf, mvsn = mvsf_ring.next()
            dma_in(ksf, cache_swa_k[bi], [ksn], ksn)
            dma_in(vsf, cache_swa_v[bi], [vsn], vsn)
            dma_in(mksf, cache_mem_k[bi].rearrange("(m p) f -> p m f", p=128), [mksn], mksn)
            dma_in(mvsf, cache_mem_v[bi].rearrange("(m p) f -> p m f", p=128), [mvsn], mvsn)
            vx, vxn = vx_ring.next()
            P.add("sp", (lambda vx, bi: lambda h: h.dma_start(out=vx, in_=kvout[bi:bi + 1, 128:256]))(vx, bi), reads=["s_kvtok"], writes=[vxn], dma_key=vxn)
            kdup, kdn = kdup_ring.next()
            slot = bi % 5
            vaug_n = ("s_vaug", slot)
            ksv = ksf.rearrange("p (k d) -> p k d", k=2)
            vsv = vsf.rearrange("p (k d) -> p k d", k=2)
            copy("pool", kdup[:, :, 0:64], ksv, [ksn], [kdn])
            copy("pool", kdup[:, :, 64:128], ksv, [ksn, kdn], [kdn])
            copy("pool", vaug[:, slot, :, 0, 0:64], vsv, [vsn], [vaug_n])
            copy("pool", vaug[:, slot, :, 1, 64:128], vsv, [vsn, vaug_n], [vaug_n])
            vxa, vxan = vxa_ring.next()
            vxv = vx.rearrange("o (k d) -> o k d", k=2)
            copy("pool", vxa[:, :, 0, 0:64], vxv, [vxn], [vxan])
            copy("pool", vxa[:, :, 1, 64:128], vxv, [vxn, vxan], [vxan])
            mkb, mkbn = xn_ring.next()
            mkbv = mkb[:, 0:512].rearrange("p (m f) -> p m f", m=2)
            copy("pool", mkbv, mksf, [mksn], [mkbn])
            mkT_b, mkT_n = mkT_slots[bi % 2]
            mvaug_b, mvaug_n = mvaug_slots[bi % 2]
            mva = mvaug_b.rearrange("p m (a two) c -> p m a two c", two=2)
            mvs = mvsf.rearrange("p m (a two d) -> p m a two d", two=2, d=64)
            for pm in range(2):
                cols = slice(0, 64) if pm == 0 else slice(64, 128)
                for mc in range(2):
                    copy("pool", mva[:, mc, :, pm, cols], mvs[:, mc, :, pm, :], [mvsn, mvaug_n], [mvaug_n])
            k = psb_next()
            for kv in range(2):
                P.add("pe", (lambda k, kv, kdup: lambda h: h.transpose(out=psb[:, k, kv * 128:(kv + 1) * 128], in_=kdup[:, kv, :], identity=ident[:]))(k, kv, kdup),
                      reads=[kdn, "ident"], writes=[("psb", k)])
            for mc in range(2):
                for cm in range(2):
                    P.add("pe", (lambda k, mc, cm, mkbv: lambda h: h.transpose(out=psb[:, k, 256 + cm * 256 + mc * 128: 256 + cm * 256 + (mc + 1) * 128],
                                                                             in_=mkbv[:, mc, cm * 128:(cm + 1) * 128], identity=ident[:]))(k, mc, cm, mkbv),
                          reads=[mkbn, "ident"], writes=[("psb", k)])
            ktt, kttn = pt_ring.next()
            kT = ktt[:, 0:256].rearrange("p (k s) -> p k s", k=2)
            copy("dve", ktt[:, 0:256], psb[:, k, 0:256], [("psb", k)], [kttn])
            copy("dve", mkT_b[:, :, :].rearrange("p c m -> p (c m)") if bi % 2 == 0 else mask_po[:], psb[:, k, 256:768], [("psb", k)], [mkT_n])
            pk = pair()
            for kv in range(2):
                for p in range(2):
                    pr = slice(p * 64, (p + 1) * 64)
                    mm(ps[:, pk + p, kv * 2:kv * 2 + 2], kT[pr, kv, :], qT[pr, 2 * kv:2 * kv + 2, bi], True, True,
                       [kttn, QN(2 * kv), QN(2 * kv + 1)], [PSN(pk + p)])
                    mm(ps[0:1, pk + p, 8 + kv * 2:8 + kv * 2 + 2], kdupT[pr, kv, 128 + bi:128 + bi + 1], qT[pr, 2 * kv:2 * kv + 2, bi], True, True,
                       KDN + [QN(2 * kv), QN(2 * kv + 1)], [PSN(pk + p)])
            for mh in range(4):
                cm, pm = mh // 2, mh % 2
                pr = slice(pm * 64, (pm + 1) * 64)
                for mc in range(2):
                    c0 = 16 + cm * 2 + mc
                    mm(ps[:, pk + pm, c0:c0 + 1], mkT_b[pr, cm, mc * 128:(mc + 1) * 128], qmT[pr, cm, bi:bi + 1], True, True,
                       [mkT_n, QMN(cm)], [PSN(pk + pm)])
            pmi, pmin = pt_ring.next()
            pT = pmi[:, 0:8].rearrange("p (a c) -> p a c", a=2)
            pX = pmi[0:1, 8:16].rearrange("p (a c) -> p a c", a=2)
            pM = pmi[:, 16:24].rearrange("p (a c) -> p a c", a=2)
            act(pT, ps[:, pk:pk + 2, 0:4], AF.Exp, [PSN(pk), PSN(pk + 1)], [pmin])
            act(pX, ps[0:1, pk:pk + 2, 8:12], AF.Exp, [PSN(pk), PSN(pk + 1), pmin], [pmin])
            act(pM, ps[:, pk:pk + 2, 16:20], AF.Exp, [PSN(pk), PSN(pk + 1), pmin], [pmin])
            bv = bank()
            for kv in range(2):
                for p in range(2):
                    c0 = p * 4 + kv * 2
                    mm(ps[:, bv, c0:c0 + 2], vaug[:, slot, kv, p, :], pT[:, p, kv * 2:kv * 2 + 2], True, False, [vaug_n, pmin], [PSN(bv)])
                    mm(ps[:, bv, c0:c0 + 2], vxa[0:1, kv, p, :], pX[0:1, p, kv * 2:kv * 2 + 2], False, False, [vxan, pmin], [PSN(bv)])
                    base = (kv * 2 + p) * 256
                    mm(ps[:, bv, c0:c0 + 2], e_sel[:, p, :], sink_full[:, base:base + 256].rearrange("p (j q) -> p j q", j=2)[:, :, 0], False, True,
                       ["e_sel", "sink_full"], [PSN(bv)])
            for mh in range(4):
                cm, pm = mh // 2, mh % 2
                for mc in range(2):
                    mm(ps[:, bv, 8 + mh:9 + mh], mvaug_b[:, mc, mh, :], pM[:, pm, cm * 2 + mc:cm * 2 + mc + 1], mc == 0, mc == 1, [mvaug_n, pmin], [PSN(bv)])
            rd, rdn = rd_ring.next()
            for p in range(2):
                orow = slice(p * 64, (p + 1) * 64)
                drow = slice((1 - p) * 64, (2 - p) * 64)
                recip(rd[orow, p * 4:p * 4 + 4], ps[drow, bv, p * 4:p * 4 + 4], [PSN(bv)], [rdn])
                tt("dve", omixT[orow, 0:4, bi], ps[orow, bv, p * 4:p * 4 + 4], rd[orow, p * 4:p * 4 + 4], ALU.mult, [PSN(bv), rdn],
                   [OMN(0), OMN(1), OMN(2), OMN(3)])
            for mh in range(4):
                cm, pm = mh // 2, mh % 2
                orow = slice(pm * 64, (pm + 1) * 64)
                drow = slice((1 - pm) * 64, (2 - pm) * 64)
                recip(rd[orow, 8 + mh:9 + mh], ps[drow, bv, 8 + mh:9 + mh], [PSN(bv)], [rdn])
                tt("dve", omixT[orow, 4 + cm, bi:bi + 1], ps[orow, bv, 8 + mh:9 + mh], rd[orow, 8 + mh:9 + mh], ALU.mult, [PSN(bv), rdn], [OMN(4 + cm)])
        for c in range(2):
            for j in range(3):
                dma_in_nc(s_cv[:, c, j, :], state_conv[:, j, c * 128:(c + 1) * 128].rearrange("b p -> p b"), ["s_cv"], "s_cvld%d%d" % (c, j))
            dma_in_nc(s_h0[:, c, :], state_h[:, c * 128:(c + 1) * 128].rearrange("b p -> p b"), ["s_h0"], "s_h0ld%d" % c)
        s_ubf = kd_hist[:, :, 0:n]
        for c in range(2):
            ts("dve", s_u[:, c, :], s_cv[:, c, 0, :], convw[:, c, 0:1], rgc[:, 0, c:c + 1], ALU.mult, ALU.add, ["s_cv", ("convw", c), ("rgc", 0)], [("s_u", c)])
            for j in range(1, 3):
                stt(s_u[:, c, :], s_cv[:, c, j, :], convw[:, c, j:j + 1], s_u[:, c, :], ALU.mult, ALU.add, ["s_cv", ("convw", c), ("s_u", c)], [("s_u", c)])
            stt(s_u[:, c, :], s_xr[:, c, :], convw[:, c, 3:4], s_u[:, c, :], ALU.mult, ALU.add, ["s_xr", ("convw", c), ("s_u", c)], [("s_u", c)])
            act(s_ubf[:, c, :], s_u[:, c, :], AF.Copy, [("s_u", c)], [("s_ub", c)])
            ba = bank()
            mm(ps[:, ba, 0:n], wbd[:, 0, c, :], s_ubf[:, c, :], True, True, ["wbd", ("s_ub", c)], [PSN(ba)])
            bx = bank()
            mm(ps[:, bx, 0:n], wbd[:, 1, c, :], s_ubf[:, c, :], True, True, ["wbd", ("s_ub", c)], [PSN(bx)])
            act(s_r[:, c, :], ps[:, ba, 0:n], AF.Sigmoid, [PSN(ba), ("rgc", 1)], [("s_r", c)], bias=rgc[:, 1, c:c + 1])
            act(s_i[:, c, :], ps[:, bx, 0:n], AF.Sigmoid, [PSN(bx), ("rgc", 2)], [("s_i", c)], bias=rgc[:, 2, c:c + 1])
            act(s_a[:, c, :], s_r[:, c, :], AF.Exp, [("s_r", c), ("rgc", 4)], [("s_a", c)], scale=rgc[:, 4, c:c + 1])
            act(s_r[:, c, :], s_r[:, c, :], AF.Exp, [("s_r", c), ("rgc", 5)], [("s_r", c)], scale=rgc[:, 5, c:c + 1])
            act(s_r[:, c, :], s_r[:, c, :], AF.Sqrt, [("s_r", c)], [("s_r", c)], scale=-1.0, bias=1.0)
            tt("dve", s_i[:, c, :], s_i[:, c, :], s_u[:, c, :], ALU.mult, [("s_i", c), ("s_u", c)], [("s_i", c)])
            tt("dve", s_i[:, c, :], s_i[:, c, :], s_r[:, c, :], ALU.mult, [("s_i", c), ("s_r", c)], [("s_i", c)])
            tt("dve", s_h[:, c, :], s_a[:, c, :], s_h0[:, c, :], ALU.mult, [("s_a", c), "s_h0"], [("s_h", c)])
            tt("dve", s_h[:, c, :], s_h[:, c, :], s_i[:, c, :], ALU.add, [("s_h", c), ("s_i", c)], [("s_h", c)])
            tt("dve", omixT[:, 6 + c, 0:n], s_h[:, c, :], s_g[:, c, :], ALU.mult, [("s_h", c), "s_g"], [OMN(6 + c)])
            dma_out(o_h_s[:, c * 128:(c + 1) * 128].rearrange("b p -> p b"), s_h[:, c, :], [("s_h", c)], "st_hs%d" % c, nonc=True)
            dma_out(o_conv_s[:, 2, c * 128:(c + 1) * 128].rearrange("b p -> p b"), s_xr[:, c, :], ["s_xr"], "st_cvs%d" % c, nonc=True)
        om_names = [OMN(i) for i in range(8)]
        pk = pair()
        for half in range(2):
            for ch in range(8):
                mm(ps[0:n, pk + half, :], omixT[:, ch, 0:n], wmo[:, ch, half * 512:(half + 1) * 512],
                   ch == 0, ch == 7, om_names + ALLBIG + [("wmo", 0), ("wmo", 1)], [PSN(pk + half)])
        postnorm_residual(pk, 0, n, 1, False, xbuf=x_s[:], xname="x_s")

    if stop_after == "const":
        P.run()
        return nc
    if do_sample:
        P.add("act", lambda h: h.dma_start(out=x_s[:], in_=x_sample), writes=["x_s", "cst_chain"], dma_key="cst")
    mem_kv_prompt()
    if stop_after == "memkv":
        P.run()
        return nc
    def switch_to_cached():
        flush_stores(0)
        xs = []
        for i in range(2):
            sv = stage[i][:].bitcast(BF16)
            for hlf in range(2):
                xs.append(sv[:, hlf * 4096:(hlf + 1) * 4096].rearrange("p (k f) -> p k f", k=8))
        items = list(wslab) + xs

        class WR:
            def __init__(self):
                self.i = 0

            def next(self):
                k = self.i % 6
                self.i += 1
                return items[k], (("wslab", k) if k < 2 else ("wslabx", k - 2))
        wsr[0] = WR()
        cached_mode[0] = True

    pre_state = {}

    def pre_sub(gi):
        return lambda s: prenorm_sub(s, x_res[:, s, :], ("x", s), 128, gi, xnT, ("xnT", s))

    def pre_hooks(gi):
        def ha(s):
            pre_state[s] = prenorm_a(x_res[:, s, :], ("x", s), 128)

        def hb(s):
            prenorm_sub(s, x_res[:, s, :], ("x", s), 128, gi, xnT, ("xnT", s), pre=pre_state.pop(s))
        return ha, hb

    for ti in range(n_ptiles):
        t0 = ti * NT
        last = ti == n_ptiles - 1
        npf = 2 if ti == 0 else 5
        if ti == 0:
            for s in range(4):
                dma_in(x_res[:, s, :], x_prompt[t0 + s * 128:t0 + (s + 1) * 128, :], [("x", s)], ("xld", s))
            for s in range(4):
                pre_sub(0)(s)
        ride = last and do_sample
        ffn(0, 0, 0, NT, 4, 128, do_pre=False, after_sub=pre_hooks(1), ride=ride, before_out=prefetch_q)
        mixer_prompt(ti, last, do_pre=False, after_sub=pre_hooks(2), at_start=lambda: prefetch_slabs(1, npf))
        if ride:
            mixer_sample(reload_weights=False)

        def after2a(s, t0=t0, last=last):
            dma_out(y_prompt[t0 + s * 128:t0 + (s + 1) * 128, :], x_res[:, s, :], [("x", s)], ("yst", s))
            if not last:
                dma_in(x_res[:, s, :], x_prompt[t0 + NT + s * 128:t0 + NT + (s + 1) * 128, :], [("x", s)], ("xld", s))
                pre_state[s] = prenorm_a(x_res[:, s, :], ("x", s), 128)

        def after2b(s, last=last):
            if not last:
                prenorm_sub(s, x_res[:, s, :], ("x", s), 128, 0, xnT, ("xnT", s), pre=pre_state.pop(s))
        after2 = (after2a, after2b)
        def before2(ti=ti):
            if ti == 0:
                switch_to_cached()
            prefetch_slabs(0, 5)
        ffn(1, 2, 2, NT, 4, 128, do_pre=False, after_sub=after2, before_out=before2 if not last else None, ride=ride)
        if ride:
            dma_out(y_sample, x_s[:], ["x_s"], "yst_s")
    run_deferred()
    P.run()
    return nc


_NC_CACHE = {}


def _in_maps(inputs):
    f = lambda a: np.ascontiguousarray(np.asarray(a, dtype=np.float32))
    shared = {}
    for k in ["ln_ffn1_pre", "ln_ffn1_post", "w_ffn1_in", "w_ffn1_out", "ln_mix_pre", "ln_mix_post", "w_in", "w_out",
              "swa_sinks", "conv_w", "conv_b", "rg_wa", "rg_ba", "rg_wx", "rg_bx", "rg_lambda", "ln_mem", "w_mem_kv",
              "ln_ffn2_pre", "ln_ffn2_post", "w_ffn2_in", "w_ffn2_out"]:
        a = f(inputs[k])[0]
        if k in ("rg_ba", "rg_bx"):
            a = a.reshape(256)
        shared[k] = np.ascontiguousarray(a)
    maps = []
    for c in range(NCORES):
        sl = slice(c * DB, (c + 1) * DB)
        m = dict(shared)
        m["x_prompt"] = f(inputs["x_prompt"][c])
        m["x_sample"] = f(inputs["x_sample"][sl, 0])
        m["mem_prompt"] = f(inputs["mem_prompt"][c])
        m["cache_swa_k"] = f(inputs["cache_swa_k"][0, sl]).reshape(DB, 128, 128)
        m["cache_swa_v"] = f(inputs["cache_swa_v"][0, sl]).reshape(DB, 128, 128)
        m["cache_mem_k"] = f(inputs["cache_mem_k"][0, sl]).reshape(DB, 256, 256)
        m["cache_mem_v"] = f(inputs["cache_mem_v"][0, sl]).reshape(DB, 256, 256)
        m["state_conv"] = f(inputs["state_conv"][0, sl])
        m["state_rglru_h"] = f(inputs["state_rglru_h"][0, sl])
        maps.append(m)
    return maps


def _gather(res):
    R = res.results
    cat = lambda k: np.stack([np.asarray(R[c][k]) for c in range(NCORES)])
    y_prompt = cat("y_prompt")
    y_sample = cat("y_sample").reshape(128, 1, D)
    swa_k_p = cat("swa_k_p").reshape(1, 8, 128, 2, 64)
    swa_v_p = cat("swa_v_p").reshape(1, 8, 128, 2, 64)
    mem_k_p = cat("mem_k_p").reshape(1, 8, 256, 4, 64)
    mem_v_p = cat("mem_v_p").reshape(1, 8, 256, 4, 64)
    conv_p = cat("conv_p").reshape(1, 8, 3, 256)
    h_p = cat("h_p").reshape(1, 8, 256)
    swa_k_s = cat("swa_k_s").reshape(1, 128, 128, 2, 64)
    swa_v_s = cat("swa_v_s").reshape(1, 128, 128, 2, 64)
    conv_s = cat("conv_s").reshape(1, 128, 3, 256)
    h_s = cat("h_s").reshape(1, 128, 256)
    return tuple(np.ascontiguousarray(a, dtype=np.float32) for a in
                 (y_prompt, y_sample, swa_k_p, swa_v_p, mem_k_p, mem_v_p, conv_p, h_p, swa_k_s, swa_v_s, conv_s, h_s))


def kernel(**inputs):
    if "nc" not in _NC_CACHE:
        _NC_CACHE["nc"] = build()
    nc = _NC_CACHE["nc"]
    res = run_bass_kernel_spmd(nc, _in_maps(inputs), core_ids=list(range(NCORES)))
    return _gather(res)
```
